# Optimizing a Trainium2 kernel written in Bass

```python
import jax, jax.numpy as jnp
from jax import lax
import numpy as np

D_MODEL = 1024
BATCH = 1
SEQ = 16384
DEPTH = 1

HEAD_DIM = 64
N_MOBA_HEADS = 6
N_DIL_HEADS = 6
N_MEM_HEADS = 4
N_MEM = 256
MOBA_BLOCK = 256
MOBA_TOPK = 3
MOBA_QCHUNK = 128
DIL_PATTERNS = ((128, 1), (512, 4), (2048, 16))
DIL_BLOCK = 128
D_FF = 2816
CONV_WIDTH = 3
ROPE_THETA = 10000.0
EPS = 1e-6
N_BRANCHES = 3
MOBA_W = N_MOBA_HEADS * HEAD_DIM
DIL_W = N_DIL_HEADS * HEAD_DIM
MEM_W = N_MEM_HEADS * HEAD_DIM
IN_COLS = 3 * MOBA_W + 3 * DIL_W + MEM_W + N_BRANCHES * D_MODEL

kernel_name = "hybrid_moba_dilated_memory_convffn"


def rmsnorm(x, g):
    x32 = x.astype(jnp.float32)
    y = x32 * lax.rsqrt(jnp.mean(x32 * x32, axis=-1, keepdims=True) + EPS)
    return (y * g.astype(jnp.float32)).astype(x.dtype)


def rope(x, positions):
    half = HEAD_DIM // 2
    inv_freq = ROPE_THETA ** (-jnp.arange(half, dtype=jnp.float32) / half)
    ang = positions.astype(jnp.float32)[..., None] * inv_freq
    cos = jnp.cos(ang)[:, :, None, :]
    sin = jnp.sin(ang)[:, :, None, :]
    x1 = x[..., :half].astype(jnp.float32)
    x2 = x[..., half:].astype(jnp.float32)
    out = jnp.concatenate([x1 * cos - x2 * sin, x1 * sin + x2 * cos], axis=-1)
    return out.astype(x.dtype)


def moba_attention(q, k, v):
    B, S, H, Dh = q.shape
    s_pad = -(-S // MOBA_BLOCK) * MOBA_BLOCK
    pad = s_pad - S
    q, k, v = [jnp.pad(t, ((0, 0), (0, pad), (0, 0), (0, 0))).transpose(0, 2, 1, 3) for t in (q, k, v)]
    nb = s_pad // MOBA_BLOCK
    topk = min(MOBA_TOPK, nb)
    kb = k.reshape(B, H, nb, MOBA_BLOCK, Dh)
    vb = v.reshape(B, H, nb, MOBA_BLOCK, Dh)
    k_mean = jnp.mean(kb.astype(jnp.float32), axis=3)
    scale = Dh ** -0.5
    bi = jnp.arange(B)[:, None, None, None]
    hi = jnp.arange(H)[None, :, None, None]
    blk_ids = jnp.arange(nb)

    def one_chunk(c):
        q0 = c * MOBA_QCHUNK
        qc = lax.dynamic_slice_in_dim(q, q0, MOBA_QCHUNK, axis=2)
        qpos = q0 + jnp.arange(MOBA_QCHUNK)
        own = q0 // MOBA_BLOCK
        gate = jnp.einsum('bhqd,bhnd->bhqn', qc.astype(jnp.float32), k_mean)
        gate = jnp.where(blk_ids < own, gate, -jnp.inf)
        gval, sel = lax.top_k(gate, topk)
        sel_ok = jnp.isfinite(gval)
        k_sel = kb[bi, hi, sel]
        v_sel = vb[bi, hi, sel]
        s_sel = jnp.einsum('bhqd,bhqnjd->bhqnj', qc, k_sel).astype(jnp.float32) * scale
        s_sel = jnp.where(sel_ok[..., None], s_sel, -jnp.inf)
        s_sel = s_sel.reshape(B, H, MOBA_QCHUNK, topk * MOBA_BLOCK)
        k_own = lax.dynamic_slice_in_dim(k, own * MOBA_BLOCK, MOBA_BLOCK, axis=2)
        v_own = lax.dynamic_slice_in_dim(v, own * MOBA_BLOCK, MOBA_BLOCK, axis=2)
        kpos = own * MOBA_BLOCK + jnp.arange(MOBA_BLOCK)
        s_own = jnp.einsum('bhqd,bhjd->bhqj', qc, k_own).astype(jnp.float32) * scale
        s_own = jnp.where(kpos[None, :] <= qpos[:, None], s_own, -jnp.inf)
        p = jax.nn.softmax(jnp.concatenate([s_sel, s_own], axis=-1), axis=-1)
        p_sel = p[..., :topk * MOBA_BLOCK].reshape(B, H, MOBA_QCHUNK, topk, MOBA_BLOCK)
        p_own = p[..., topk * MOBA_BLOCK:]
        o = (jnp.einsum('bhqnj,bhqnjd->bhqd', p_sel.astype(v.dtype), v_sel)
             + jnp.einsum('bhqj,bhjd->bhqd', p_own.astype(v.dtype), v_own))
        return o

    outs = lax.map(one_chunk, jnp.arange(s_pad // MOBA_QCHUNK))
    o = outs.transpose(1, 0, 3, 2, 4).reshape(B, s_pad, H, Dh)
    return o[:, :S]


def dilated_attention(q, k, v):
    B, S, H, Dh = q.shape
    max_dil = max(d for _, d in DIL_PATTERNS)
    unit = max_dil * DIL_BLOCK
    s_pad = -(-S // unit) * unit
    pad = s_pad - S
    q, k, v = [jnp.pad(t, ((0, 0), (0, pad), (0, 0), (0, 0))) for t in (q, k, v)]
    scale = Dh ** -0.5
    outs, lses = [], []
    for window, dil in DIL_PATTERNS:
        span = window // dil
        L = s_pad // dil
        nblk = L // DIL_BLOCK

        def to_sub(t):
            return t.reshape(B, L, dil, H, Dh).transpose(0, 2, 1, 3, 4).reshape(B, dil, nblk, DIL_BLOCK, H, Dh)

        def from_sub(t):
            rest = t.shape[4:]
            t = t.reshape((B, dil, L) + rest)
            t = jnp.moveaxis(t, 1, 2)
            return t.reshape((B, s_pad) + rest)

        def with_prev(t):
            prev = jnp.pad(t[:, :, :-1], ((0, 0), (0, 0), (1, 0), (0, 0), (0, 0), (0, 0)))
            return jnp.concatenate([prev, t], axis=3)

        qs = to_sub(q)
        ks = with_prev(to_sub(k))
        vs = with_prev(to_sub(v))
        s = jnp.einsum('brnqhd,brnkhd->brnhqk', qs, ks).astype(jnp.float32) * scale
        qi = jnp.arange(DIL_BLOCK)[:, None] + DIL_BLOCK
        kj = jnp.arange(2 * DIL_BLOCK)[None, :]
        dist = qi - kj
        band = (dist >= 0) & (dist <= span)
        not_before_start = (jnp.arange(nblk) > 0)[:, None, None] | (kj >= DIL_BLOCK)[None]
        mask = band[None] & not_before_start
        s = jnp.where(mask[None, None, :, None], s, -jnp.inf)
        m = jnp.max(s, axis=-1, keepdims=True)
        p = jnp.exp(s - m)
        den = jnp.sum(p, axis=-1)
        o = jnp.einsum('brnhqk,brnkhd->brnqhd', p.astype(v.dtype), vs)
        o = o / den.transpose(0, 1, 2, 4, 3)[..., None]
        lse = (m[..., 0] + jnp.log(den)).transpose(0, 1, 2, 4, 3)
        outs.append(from_sub(o))
        lses.append(from_sub(lse))
    w = jax.nn.softmax(jnp.stack(lses, axis=0), axis=0)
    o = jnp.einsum('pbsh,pbshd->bshd', w, jnp.stack(outs, axis=0).astype(jnp.float32))
    return o[:, :S].astype(q.dtype)


def memory_attention(q, mem_k, mem_v):
    s = jnp.einsum('bshd,bmhd->bhsm', q, mem_k).astype(jnp.float32) * (HEAD_DIM ** -0.5)
    p = jax.nn.softmax(s, axis=-1)
    return jnp.einsum('bhsm,bmhd->bshd', p.astype(mem_v.dtype), mem_v)


def causal_depthwise_conv(u, w, b):
    S = u.shape[1]
    u_pad = jnp.pad(u, ((0, 0), (CONV_WIDTH - 1, 0), (0, 0)))
    y = b
    for j in range(CONV_WIDTH):
        y = y + w[j] * u_pad[:, j:j + S]
    return y


def setup_inputs(seed: int = 0) -> dict:
    key = jax.random.key(seed)
    ks = jax.random.split(key, 24)
    f32 = jnp.float32

    def nrm(k, shape, scale):
        return jax.random.normal(k, shape, f32) * scale

    def gain(k, shape):
        return 1.0 + 0.02 * jax.random.normal(k, shape, f32)

    return {
        "x": nrm(ks[0], (BATCH, SEQ, D_MODEL), 1.0),
        "mem": nrm(ks[1], (BATCH, N_MEM, D_MODEL), 1.0),
        "positions": jnp.broadcast_to(jnp.arange(SEQ, dtype=jnp.int32)[None], (BATCH, SEQ)),
        "mix_norm_g": gain(ks[2], (DEPTH, D_MODEL)),
        "mem_norm_g": gain(ks[3], (DEPTH, D_MODEL)),
        "w_in": nrm(ks[4], (DEPTH, D_MODEL, IN_COLS), D_MODEL ** -0.5),
        "moba_q_norm_g": gain(ks[5], (DEPTH, HEAD_DIM)),
        "moba_k_norm_g": gain(ks[6], (DEPTH, HEAD_DIM)),
        "dil_q_norm_g": gain(ks[7], (DEPTH, HEAD_DIM)),
        "dil_k_norm_g": gain(ks[8], (DEPTH, HEAD_DIM)),
        "mem_q_norm_g": gain(ks[9], (DEPTH, HEAD_DIM)),
        "mem_k_norm_g": gain(ks[10], (DEPTH, HEAD_DIM)),
        "w_mem_kv": nrm(ks[11], (DEPTH, D_MODEL, 2 * MEM_W), D_MODEL ** -0.5),
        "w_branch_moba": nrm(ks[12], (DEPTH, MOBA_W, D_MODEL), MOBA_W ** -0.5),
        "w_branch_dil": nrm(ks[13], (DEPTH, DIL_W, D_MODEL), DIL_W ** -0.5),
        "w_branch_mem": nrm(ks[14], (DEPTH, MEM_W, D_MODEL), MEM_W ** -0.5),
        "w_out": nrm(ks[15], (DEPTH, D_MODEL, D_MODEL), D_MODEL ** -0.5),
        "ffn_norm_g": gain(ks[16], (DEPTH, D_MODEL)),
        "w_ffn_up": nrm(ks[17], (DEPTH, D_MODEL, 2 * D_FF), D_MODEL ** -0.5),
        "ffn_conv_w": nrm(ks[18], (DEPTH, CONV_WIDTH, 2 * D_FF), CONV_WIDTH ** -0.5),
        "ffn_conv_b": nrm(ks[19], (DEPTH, 2 * D_FF), 0.01),
        "w_ffn_down": nrm(ks[20], (DEPTH, D_FF, D_MODEL), D_FF ** -0.5),
    }


def reference(x, mem, positions, mix_norm_g, mem_norm_g, w_in, moba_q_norm_g, moba_k_norm_g,
              dil_q_norm_g, dil_k_norm_g, mem_q_norm_g, mem_k_norm_g, w_mem_kv,
              w_branch_moba, w_branch_dil, w_branch_mem, w_out, ffn_norm_g,
              w_ffn_up, ffn_conv_w, ffn_conv_b, w_ffn_down):
    B, S, _ = x.shape
    split_at = [int(c) for c in np.cumsum([MOBA_W, MOBA_W, MOBA_W, DIL_W, DIL_W, DIL_W, MEM_W])]
    for l in range(DEPTH):
        h = rmsnorm(x, mix_norm_g[l])
        proj = h @ w_in[l]
        qa, ka, va, qd, kd, vd, qm, graw = jnp.split(proj, split_at, axis=-1)
        heads = lambda t, n: t.reshape(B, S, n, HEAD_DIM)
        qa = rope(rmsnorm(heads(qa, N_MOBA_HEADS), moba_q_norm_g[l]), positions)
        ka = rope(rmsnorm(heads(ka, N_MOBA_HEADS), moba_k_norm_g[l]), positions)
        va = heads(va, N_MOBA_HEADS)
        qd = rope(rmsnorm(heads(qd, N_DIL_HEADS), dil_q_norm_g[l]), positions)
        kd = rope(rmsnorm(heads(kd, N_DIL_HEADS), dil_k_norm_g[l]), positions)
        vd = heads(vd, N_DIL_HEADS)
        qm = rmsnorm(heads(qm, N_MEM_HEADS), mem_q_norm_g[l])
        gates = jax.nn.sigmoid(graw.astype(jnp.float32)).astype(x.dtype).reshape(B, S, N_BRANCHES, D_MODEL)

        mem_n = rmsnorm(mem, mem_norm_g[l])
        mkv = (mem_n @ w_mem_kv[l]).reshape(B, N_MEM, 2, N_MEM_HEADS, HEAD_DIM)
        mk = rmsnorm(mkv[:, :, 0], mem_k_norm_g[l])
        mv = mkv[:, :, 1]

        o_a = moba_attention(qa, ka, va).reshape(B, S, MOBA_W) @ w_branch_moba[l]
        o_d = dilated_attention(qd, kd, vd).reshape(B, S, DIL_W) @ w_branch_dil[l]
        o_m = memory_attention(qm, mk, mv).reshape(B, S, MEM_W) @ w_branch_mem[l]
        merged = gates[:, :, 0] * o_a + gates[:, :, 1] * o_d + gates[:, :, 2] * o_m
        x = x + merged @ w_out[l]

        h2 = rmsnorm(x, ffn_norm_g[l])
        u = causal_depthwise_conv(h2 @ w_ffn_up[l], ffn_conv_w[l], ffn_conv_b[l])
        u_gate, u_val = jnp.split(u, 2, axis=-1)
        x = x + (jax.nn.silu(u_gate) * u_val) @ w_ffn_down[l]
    return x
```

```python
import math
from contextlib import ExitStack

import ml_dtypes
import numpy as np

import concourse.bass as bass
import concourse.mybir as mybir
from concourse.bass_utils import run_bass_kernel_spmd

F32 = mybir.dt.float32
BF16 = mybir.dt.bfloat16
I32 = mybir.dt.int32
ALU = mybir.AluOpType
AF = mybir.ActivationFunctionType
AX = mybir.AxisListType

S = 16384
D = 1024
NCORE = 8
TOK = S // NCORE
NQT = 17
NQ = NQT * 128
NWT = 33
NW = NWT * 128
NOT = 18
NO = NOT * 128
DFF = 2816
EPS = 1e-6
NEG = -30000.0
NDS = 12
DEBUG_OUT = set()


class Res:
    __slots__ = ("w", "r")

    def __init__(self):
        self.w = None
        self.r = {}


class Plan:
    ENG = ("pe", "act", "dve", "pool", "sp")

    def __init__(self):
        self.prog = {e: [] for e in self.ENG}
        self.cnt = {e: 0 for e in self.ENG}
        self.seen = {e: {} for e in self.ENG}
        self.dq = {q: {"i": 0, "val": [0] * NDS} for q in ("sp", "act", "pool")}

    def _deps(self, eng, reads, writes):
        need = {}

        def add(tok):
            if tok is None:
                return
            k, v = tok
            if need.get(k, 0) < v:
                need[k] = v

        for r in reads:
            add(r.w)
        for w in writes:
            add(w.w)
            for t in w.r.values():
                add(t)
        out = []
        for k, v in need.items():
            if eng == "pe" and k == "pe":
                continue
            if self.seen[eng].get(k, 0) < v:
                self.seen[eng][k] = v
                out.append((k, v))
        return out

    def op(self, eng, fn, reads=(), writes=()):
        waits = self._deps(eng, reads, writes)
        self.cnt[eng] += 1
        tok = (eng, self.cnt[eng])
        self.prog[eng].append((waits, fn, None))
        for r in reads:
            r.r[eng] = tok
        for w in writes:
            w.w = tok
            w.r = {}

    def dma(self, q, fn, reads=(), writes=()):
        waits = self._deps(q, reads, writes)
        d = self.dq[q]
        i = d["i"] % NDS
        d["i"] += 1
        key = f"d_{q}_{i}"
        prev = d["val"][i]
        if prev > 0 and self.seen[q].get(key, 0) < prev:
            self.seen[q][key] = prev
            waits.append((key, prev))
        d["val"][i] = prev + 16
        tok = (key, prev + 16)
        self.prog[q].append((waits, fn, key))
        for r in reads:
            r.r[key] = tok
        for w in writes:
            w.w = tok
            w.r = {}


def plan_barrier(P):
    for e in P.ENG:
        waits = []
        for k2 in P.ENG:
            v = P.cnt[k2]
            if v > 0 and P.seen[e].get(k2, 0) < v:
                P.seen[e][k2] = v
                waits.append((k2, v))
        for q in P.dq:
            for i in range(NDS):
                v = P.dq[q]["val"][i]
                key = f"d_{q}_{i}"
                if v > 0 and P.seen[e].get(key, 0) < v:
                    P.seen[e][key] = v
                    waits.append((key, v))
        if waits:
            P.prog[e].append((waits, None, None))


def run_pipeline(gens, maxd=3):
    active = []
    for g in gens:
        while len(active) >= maxd:
            for a in list(active):
                if next(a, "done") == "done":
                    active.remove(a)
        for a in list(active):
            if next(a, "done") == "done":
                active.remove(a)
        active.append(g)
        if next(g, "done") == "done":
            active.remove(g)
    while active:
        for a in list(active):
            if next(a, "done") == "done":
                active.remove(a)


class Ring:
    def __init__(self, tiles):
        self.t = tiles
        self.res = [Res() for _ in tiles]
        self.i = 0

    def next(self):
        k = self.i % len(self.t)
        self.i += 1
        return self.t[k], self.res[k]


def build_nc():
    nc = bass.Bass("TRN2", target_bir_lowering=False)
    P = Plan()
    es = ExitStack()

    def din(name, shape, dt=F32):
        return nc.dram_tensor(name, list(shape), dt, kind="ExternalInput").ap()

    def dscr(name, shape, dt=BF16):
        if name in DEBUG_OUT:
            return nc.dram_tensor(name, list(shape), dt, kind="ExternalOutput").ap()
        return nc.dram_tensor(name, list(shape), dt).ap()

    xall = din("xall", [S, D])
    xwin = din("xwin", [NW, D])
    pos_allT = din("pos_allT", [128, S // 128], I32)
    pos_winT = din("pos_winT", [128, NWT], I32)
    mem = din("mem", [256, D])
    w_in = din("w_in", [D, 5632])
    w_mkv = din("w_mkv", [D, 512])
    w_ba = din("w_ba", [384, D])
    w_bd = din("w_bd", [384, D])
    w_bm = din("w_bm", [256, D])
    w_out = din("w_out", [D, D])
    w_up = din("w_up", [D, 5632])
    w_dn = din("w_dn", [DFF, D])
    cw = din("cw", [128, 44, 4])
    g_mix = din("g_mix", [128, D])
    g_mem = din("g_mem", [128, D])
    g_ffn = din("g_ffn", [128, D])
    g6 = din("g6", [128, 6, 64])
    c_ident = din("c_ident", [128, 128], BF16)
    c_tri = din("c_tri", [128, 128], BF16)
    c_mdil = din("c_mdil", [128, 17, 128], BF16)
    c_tmat = din("c_tmat", [64, S], BF16)
    c_invf = din("c_invf", [128, 64])
    c_offs = din("c_offs", [128, 64])
    c_gmask = din("c_gmask", [128, NQT, 64])
    c_kvalid = din("c_kvalid", [128, NWT])
    c_hvalid = din("c_hvalid", [128, 1])
    out_d = nc.dram_tensor("out", [TOK, D], F32, kind="ExternalOutput").ap()

    KaT_d = dscr("KaT_d", [6, 64, S])
    Va_d = dscr("Va_d", [S, 384])
    KaTo_d = dscr("KaTo_d", [6, 64, NO])
    Vao_d = dscr("Vao_d", [NO, 384])
    KdT_d = dscr("KdT_d", [6, 64, NW])
    Vd_d = dscr("Vd_d", [NW, 384])
    QaT_d = dscr("QaT_d", [6, 64, NQ])
    QdT_d = dscr("QdT_d", [6, 64, NQ])
    QmT_d = dscr("QmT_d", [4, 64, NQ])
    sig_d = dscr("sig_d", [24, 128, NQ])
    MkT_d = dscr("MkT_d", [4, 64, 256])
    Mv_d = dscr("Mv_d", [256, 256])
    x1_d = dscr("x1_d", [NQ, D], F32)

    def sb(name, shape, dt):
        return es.enter_context(nc.sbuf_tensor(name, list(shape), dt))

    def ring(name, shape, dt, n, stack=None):
        st = stack or es
        return Ring([st.enter_context(nc.sbuf_tensor(f"{name}{i}", list(shape), dt)) for i in range(n)])

    psd = [es.enter_context(nc.psum_tensor(f"psd{i}", [128, 1024], F32)) for i in range(4)]
    ps = [psd[i // 2][:, (i % 2) * 512:(i % 2 + 1) * 512] for i in range(8)]
    psr = [Res() for _ in range(8)]

    ident = sb("ident", [128, 128], BF16)
    tri = sb("tri", [128, 128], BF16)
    invf = sb("invf", [128, 64], F32)
    offs = sb("offs", [128, 64], F32)
    g6s = sb("g6s", [128, 6, 64], F32)
    gmix = sb("gmix", [128, D], F32)
    cres = Res()
    fence = sb("fence", [128, 2], F32)
    P.op("dve", lambda e: e.memset(fence[:], 0.0), writes=[Res()])
    epsc = sb("epsc", [128, 1], F32)
    P.op("dve", lambda e: e.memset(epsc[:], EPS), writes=[cres])
    for dst, src in ((ident, c_ident), (tri, c_tri), (invf, c_invf), (offs, c_offs), (g6s, g6), (gmix, g_mix)):
        P.dma("sp", (lambda d, s_: lambda e: e.dma_start(out=d[:], in_=s_))(dst, src), writes=[cres])

    def load_w(wsb, wres, wsrc, r0, r1, c0, c1, col_off=0):
        nk = (r1 - r0) // 128
        for kf in range(nk):
            P.dma("pool", (lambda kf=kf: lambda e: e.dma_start(
                out=wsb[:, kf, col_off:col_off + (c1 - c0)],
                in_=wsrc[r0 + kf * 128:r0 + (kf + 1) * 128, c0:c1]))(), writes=[wres])

    def rms_rstd(stack_tiles, src_ap, src_res, n, width, split):
        sq, ss, rs = stack_tiles
        sqt, sqr = sq.next()
        sst, ssr = ss.next()
        rst, rsr = rs.next()
        P.op("act", lambda e: e.activation(out=sqt[:, 0:n * width], in_=src_ap, func=AF.Square),
             reads=[src_res], writes=[sqr])
        if split:
            yield
        P.op("dve", lambda e: e.tensor_reduce(out=sst[:, 0:n], in_=sqt[:, 0:n * width].rearrange("p (n w) -> p n w", w=width),
                                              axis=AX.X, op=ALU.add), reads=[sqr], writes=[ssr])
        if split:
            yield
        P.op("act", lambda e: e.activation(out=sst[:, 0:n], in_=sst[:, 0:n], func=AF.Sqrt, scale=1.0 / width, bias=epsc[:, 0:1]),
             reads=[ssr, cres], writes=[ssr])
        if split:
            yield
        P.op("dve", lambda e: e.reciprocal(out=rst[:, 0:n], in_=sst[:, 0:n]), reads=[ssr], writes=[rsr])
        return rst, rsr

    with ExitStack() as ph:
        RG = {}

        def set_rings(st, tag, maxd, npf, nob):
            k = maxd
            RG["xt"] = ring("xt" + tag, [128, D], F32, 3, st)
            RG["junk"] = ring("junk" + tag, [128, D], BF16, 1, st)
            RG["ssx"] = ring("ssx" + tag, [128, 1], F32, 6, st)
            RG["xn"] = ring("xn" + tag, [128, D], BF16, 3, st)
            RG["hT"] = ring("hT" + tag, [128, 8, 128], BF16, k + 2, st)
            RG["pf"] = ring("pf" + tag, [128, 512], F32, npf, st)
            RG["sq"] = ring("sq" + tag, [128, 512], F32, 4, st)
            RG["ss"] = ring("ss" + tag, [128, 8], F32, 12, st)
            RG["rs"] = ring("rs" + tag, [128, 8], F32, 12, st)
            RG["nrm"] = ring("nrm" + tag, [128, 6, 64], F32, 4, st)
            RG["rt"] = ring("rt" + tag, [128, 2, 6, 64], F32, 4, st)
            RG["ob"] = ring("ob" + tag, [128, 384], BF16, nob, st)
            RG["vob"] = ring("vob" + tag, [128, 384], BF16, 3, st)
            RG["st"] = ring("st" + tag, [128, 3, 128], BF16, 6, st)
            RG["gsb"] = ring("gsb" + tag, [128, 512], BF16, 4, st)

        def build_tabs(tabs, tabs_res, posT_src, nt, gfold, tag, barrier=True):
            with ExitStack() as tmp:
                posi = tmp.enter_context(nc.sbuf_tensor("posi" + tag, [128, nt], I32))
                posf = tmp.enter_context(nc.sbuf_tensor("posf" + tag, [128, nt], F32))
                CH = 8
                A = tmp.enter_context(nc.sbuf_tensor("bA" + tag, [128, CH, 64], F32))
                T = tmp.enter_context(nc.sbuf_tensor("bT" + tag, [128, CH, 64], F32))
                Kf = tmp.enter_context(nc.sbuf_tensor("bK" + tag, [128, CH, 64], F32))
                Ki = tmp.enter_context(nc.sbuf_tensor("bKi" + tag, [128, CH, 64], I32))
                r = Res()
                P.dma("sp", lambda e: e.dma_start(out=posi[:], in_=posT_src), writes=[r])
                P.op("dve", lambda e: e.tensor_copy(out=posf[:], in_=posi[:]), reads=[r], writes=[r])
                for c0 in range(0, nt, CH):
                    n = min(CH, nt - c0)
                    a_, t_, k_, ki_ = A[:, 0:n, :], T[:, 0:n, :], Kf[:, 0:n, :], Ki[:, 0:n, :]
                    P.op("dve", (lambda a_=a_, c0=c0, n=n: lambda e: e.tensor_tensor(
                        out=a_, in0=posf[:, c0:c0 + n].unsqueeze(2).to_broadcast([128, n, 64]),
                        in1=invf[:].unsqueeze(1).to_broadcast([128, n, 64]), op=ALU.mult))(), reads=[r, cres], writes=[r])
                    P.op("dve", (lambda a_=a_: lambda e: e.tensor_scalar(out=a_[:, :, 32:64], in0=a_[:, :, 32:64], scalar1=math.pi / 2,
                                                                         scalar2=None, op0=ALU.add))(), reads=[r], writes=[r])
                    P.op("dve", (lambda a_=a_, t_=t_: lambda e: e.tensor_scalar(out=t_, in0=a_, scalar1=1.0 / (2 * math.pi), scalar2=0.5,
                                                                                op0=ALU.mult, op1=ALU.add))(), reads=[r], writes=[r])
                    P.op("dve", (lambda ki_=ki_, t_=t_: lambda e: e.tensor_copy(out=ki_, in_=t_))(), reads=[r], writes=[r])
                    P.op("dve", (lambda ki_=ki_, k_=k_: lambda e: e.tensor_copy(out=k_, in_=ki_))(), reads=[r], writes=[r])
                    P.op("dve", (lambda ki_=ki_, k_=k_, t_=t_: lambda e: e.tensor_tensor(out=t_, in0=t_, in1=k_, op=ALU.is_lt))(),
                         reads=[r], writes=[r])
                    P.op("dve", (lambda k_=k_, t_=t_: lambda e: e.tensor_tensor(out=k_, in0=k_, in1=t_, op=ALU.subtract))(),
                         reads=[r], writes=[r])
                    P.op("dve", (lambda k_=k_, a_=a_: lambda e: e.scalar_tensor_tensor(out=a_, in0=k_, scalar=-2.0 * math.pi, in1=a_,
                                                                                       op0=ALU.mult, op1=ALU.add))(), reads=[r], writes=[r])
                    P.op("dve", (lambda a_=a_: lambda e: e.tensor_scalar(out=a_, in0=a_, scalar1=-3.141592, scalar2=3.141592,
                                                                         op0=ALU.max, op1=ALU.min))(), reads=[r], writes=[r])
                    P.op("act", (lambda a_=a_, t_=t_: lambda e: e.activation(out=t_, in_=a_, func=AF.Sin))(), reads=[r], writes=[r])
                    tb = tabs[:, c0:c0 + n, :, :]
                    if gfold is None:
                        P.op("dve", (lambda tb=tb, t_=t_: lambda e: e.tensor_copy(out=tb[:, :, 0, 0:32], in_=t_[:, :, 32:64]))(), reads=[r], writes=[tabs_res])
                        P.op("dve", (lambda tb=tb, t_=t_: lambda e: e.tensor_copy(out=tb[:, :, 0, 32:64], in_=t_[:, :, 32:64]))(), reads=[r], writes=[tabs_res])
                        P.op("dve", (lambda tb=tb, t_=t_: lambda e: e.tensor_copy(out=tb[:, :, 1, 32:64], in_=t_[:, :, 0:32]))(), reads=[r], writes=[tabs_res])
                        P.op("dve", (lambda tb=tb, t_=t_: lambda e: e.tensor_scalar(out=tb[:, :, 1, 0:32], in0=t_[:, :, 0:32], scalar1=-1.0, scalar2=None,
                                                                                   op0=ALU.mult))(), reads=[r], writes=[tabs_res])
                    else:
                        g1 = g6s[:, gfold, 0:32].unsqueeze(1).to_broadcast([128, n, 32])
                        g2 = g6s[:, gfold, 32:64].unsqueeze(1).to_broadcast([128, n, 32])
                        P.op("dve", (lambda tb=tb, t_=t_, g1=g1: lambda e: e.tensor_tensor(out=tb[:, :, 0, 0:32], in0=t_[:, :, 32:64], in1=g1, op=ALU.mult))(),
                             reads=[r, cres], writes=[tabs_res])
                        P.op("dve", (lambda tb=tb, t_=t_, g2=g2: lambda e: e.tensor_tensor(out=tb[:, :, 0, 32:64], in0=t_[:, :, 32:64], in1=g2, op=ALU.mult))(),
                             reads=[r, cres], writes=[tabs_res])
                        P.op("dve", (lambda tb=tb, t_=t_, g1=g1: lambda e: e.tensor_tensor(out=tb[:, :, 1, 32:64], in0=t_[:, :, 0:32], in1=g1, op=ALU.mult))(),
                             reads=[r, cres], writes=[tabs_res])
                        P.op("dve", (lambda tb=tb, t_=t_, g2=g2: lambda e: e.scalar_tensor_tensor(out=tb[:, :, 1, 0:32], in0=t_[:, :, 0:32], scalar=-1.0, in1=g2,
                                                                                                 op0=ALU.mult, op1=ALU.mult))(),
                             reads=[r, cres], writes=[tabs_res])
                if barrier:
                    plan_barrier(P)

        def frontend(xsrc, t, gtile=None):
            gtile = gmix if gtile is None else gtile
            xt, xtr = RG["xt"].next()
            P.dma("sp", lambda e: e.dma_start(out=xt[:], in_=xsrc[t * 128:(t + 1) * 128, :]), writes=[xtr])
            jk, jkr = RG["junk"].next()
            ssx, ssxr = RG["ssx"].next()
            P.op("act", lambda e: e.activation(out=jk[:], in_=xt[:], func=AF.Square, accum_out=ssx[:]),
                 reads=[xtr], writes=[jkr, ssxr])
            P.op("act", lambda e: e.copy(out=fence[:, 0:1], in_=fence[:, 1:2]), writes=[ssxr])
            P.op("act", lambda e: e.activation(out=ssx[:], in_=ssx[:], func=AF.Sqrt, scale=1.0 / D, bias=epsc[:, 0:1]),
                 reads=[ssxr, cres], writes=[ssxr])
            P.op("dve", lambda e: e.reciprocal(out=ssx[:], in_=ssx[:]), reads=[ssxr], writes=[ssxr])
            yield
            xn, xnr = RG["xn"].next()
            P.op("dve", lambda e: e.scalar_tensor_tensor(out=xn[:], in0=xt[:], scalar=ssx[:, 0:1], in1=gtile[:],
                                                         op0=ALU.mult, op1=ALU.mult),
                 reads=[xtr, ssxr, cres], writes=[xnr])
            yield
            for kf in range(8):
                b = kf // 4
                P.op("pe", (lambda kf=kf, b=b: lambda e: e.matmul(ps[b][:, (kf % 4) * 128:(kf % 4 + 1) * 128],
                                                                   lhsT=xn[:, kf * 128:(kf + 1) * 128], rhs=ident[:],
                                                                   start=True, stop=True))(),
                     reads=[xnr, cres], writes=[psr[b]])
            hT, hTr = RG["hT"].next()
            P.op("act", lambda e: e.copy(out=hT[:, 0:4, :].rearrange("p a b -> p (a b)"), in_=ps[0][:]),
                 reads=[psr[0]], writes=[hTr])
            P.op("dve", lambda e: e.tensor_copy(out=hT[:, 4:8, :].rearrange("p a b -> p (a b)"), in_=ps[1][:]),
                 reads=[psr[1]], writes=[hTr])
            return hT, hTr

        pbc = [0]

        def nb():
            b = 2 + pbc[0] % 6
            pbc[0] += 1
            return b

        def proj(W, Wres, hT, hTr, c0, n, bank):
            for kf in range(8):
                P.op("pe", (lambda kf=kf: lambda e: e.matmul(ps[bank][:, 0:n], lhsT=hT[:, kf, :], rhs=W[:, kf, c0:c0 + n],
                                                             start=(kf == 0), stop=(kf == 7)))(),
                     reads=[hTr, Wres], writes=[psr[bank]])

        def proj_evac(W, Wres, hT, hTr, c0, n):
            b = nb()
            proj(W, Wres, hT, hTr, c0, n, b)
            pf, pfr = RG["pf"].next()
            P.op("act", lambda e: e.copy(out=pf[:, 0:n], in_=ps[b][:, 0:n]), reads=[psr[b]], writes=[pfr])
            return pf, pfr

        def v_to_dram(bank, dst, row0, c0=0, n=384):
            ob, obr = RG["vob"].next()
            P.op("act", lambda e: e.copy(out=ob[:, 0:n], in_=ps[bank][:, c0:c0 + n]), reads=[psr[bank]], writes=[obr])
            P.dma("act", lambda e: e.dma_start(out=dst[row0:row0 + 128, :], in_=ob[:, 0:n]), reads=[obr])

        def proj_v(W, Wres, hT, hTr, c0, n, dst, row0):
            b = nb()
            proj(W, Wres, hT, hTr, c0, n, b)
            v_to_dram(b, dst, row0, 0, n)

        def head_post(pfin, nh, tabs, tabs_res, t, scale, gain_idx, rope=True, split=False):
            pf, pfr = pfin
            n = nh * 64
            rst, rsr = yield from rms_rstd((RG["sq"], RG["ss"], RG["rs"]), pf[:, 0:n], pfr, nh, 64, split)
            nm, nmr = RG["nrm"].next()
            pf3 = pf[:, 0:n].rearrange("p (h d) -> p h d", d=64)
            P.op("dve", lambda e: e.tensor_tensor(out=nm[:, 0:nh, :], in0=pf3,
                                                  in1=rst[:, 0:nh].unsqueeze(2).to_broadcast([128, nh, 64]), op=ALU.mult),
                 reads=[pfr, rsr], writes=[nmr])
            ob, obr = RG["ob"].next()
            ob3 = ob[:, 0:n].rearrange("p (h d) -> p h d", d=64)
            if not rope:
                P.op("dve", lambda e: e.scalar_tensor_tensor(
                    out=ob3, in0=nm[:, 0:nh, :], scalar=scale,
                    in1=g6s[:, gain_idx, :].unsqueeze(1).to_broadcast([128, nh, 64]), op0=ALU.mult, op1=ALU.mult),
                     reads=[nmr, cres], writes=[obr])
                return ob, obr
            if gain_idx is not None:
                P.op("dve", lambda e: e.scalar_tensor_tensor(
                    out=nm[:, 0:nh, :], in0=nm[:, 0:nh, :], scalar=scale,
                    in1=g6s[:, gain_idx, :].unsqueeze(1).to_broadcast([128, nh, 64]), op0=ALU.mult, op1=ALU.mult)
                    if False else e.tensor_tensor(out=nm[:, 0:nh, :], in0=nm[:, 0:nh, :],
                                                  in1=gsc[(gain_idx, scale)][:].unsqueeze(1).to_broadcast([128, nh, 64]), op=ALU.mult),
                     reads=[nmr, cres], writes=[nmr])
            rt, rtr = RG["rt"].next()
            cc = tabs[:, t, 0, :].unsqueeze(1).to_broadcast([128, nh, 64])
            ss_ = tabs[:, t, 1, :].unsqueeze(1).to_broadcast([128, nh, 64])
            nsw = nm[:, 0:nh, :].rearrange("p h (two d) -> p h two d", two=2)[:, :, ::-1, :]
            P.op("dve", lambda e: e.tensor_tensor(out=rt[:, 0, 0:nh, :], in0=nm[:, 0:nh, :], in1=cc, op=ALU.mult),
                 reads=[nmr, tabs_res], writes=[rtr])
            P.op("dve", lambda e: e.tensor_tensor(out=rt[:, 1, 0:nh, :].rearrange("p h (two d) -> p h two d", two=2), in0=nsw,
                                                   in1=ss_.rearrange("p h (two d) -> p h two d", two=2) if False else
                                                   tabs[:, t, 1, :].rearrange("p (two d) -> p two d", two=2).unsqueeze(1).to_broadcast([128, nh, 2, 32]),
                                                   op=ALU.mult),
                 reads=[nmr, tabs_res], writes=[rtr])
            P.op("dve", lambda e: e.tensor_tensor(out=ob3, in0=rt[:, 0, 0:nh, :], in1=rt[:, 1, 0:nh, :], op=ALU.add),
                 reads=[rtr], writes=[obr])
            return ob, obr

        def to_T_dram(ob, obr, nh, dst, col0):
            bank = nb()
            npair = nh // 2
            for hp in range(npair):
                P.op("pe", (lambda hp=hp: lambda e: e.matmul(ps[bank][:, hp * 128:(hp + 1) * 128],
                                                             lhsT=ob[:, hp * 128:(hp + 1) * 128], rhs=ident[:],
                                                             start=True, stop=True))(),
                     reads=[obr, cres], writes=[psr[bank]])
            st, str_ = RG["st"].next()
            P.op("act", lambda e: e.copy(out=st[:, 0:npair, :].rearrange("p a b -> p (a b)"), in_=ps[bank][:, 0:npair * 128]),
                 reads=[psr[bank]], writes=[str_])
            P.dma("act", lambda e: e.dma_start(
                out=dst.rearrange("(a two) d n -> (two d) a n", two=2)[:, :, col0:col0 + 128], in_=st[:, 0:npair, :]),
                reads=[str_])

        gsc = {}
        for key in ((3, 1.0), (1, 1.0), (0, 0.125), (2, 0.125)):
            gt = ph.enter_context(nc.sbuf_tensor(f"gsc{key[0]}", [128, 64], F32))
            gsc[key] = gt
            P.op("dve", (lambda gt=gt, key=key: lambda e: e.tensor_scalar(out=gt[:], in0=g6s[:, key[0], :], scalar1=key[1], scalar2=None,
                                                                         op0=ALU.mult))(), reads=[cres], writes=[cres])

        with ExitStack() as pa:
            MAXD = 10
            set_rings(pa, "A", MAXD, MAXD + 2, MAXD + 2)
            WA = pa.enter_context(nc.sbuf_tensor("WA", [128, 8, 768], BF16))
            WAres = Res()
            load_w(WA, WAres, w_in, 0, 1024, 384, 1152)
            tabsA = pa.enter_context(nc.sbuf_tensor("tabsA", [128, S // 128, 2, 64], F32))
            tabsA_res = Res()
            build_tabs(tabsA, tabsA_res, pos_allT, S // 128, 1, "A", barrier=False)

            def tileA(t):
                hT, hTr = yield from frontend(xall, t)
                yield
                pfk = proj_evac(WA, WAres, hT, hTr, 0, 384)
                proj_v(WA, WAres, hT, hTr, 384, 384, Va_d, t * 128)
                yield
                ob, obr = yield from head_post(pfk, 6, tabsA, tabsA_res, t, 1.0, None, split=True)
                yield
                to_T_dram(ob, obr, 6, KaT_d, t * 128)
            run_pipeline((tileA(t) for t in range(S // 128)), MAXD)
        plan_barrier(P)

        MAXD = 8
        set_rings(ph, "B", MAXD, MAXD + 2, MAXD + 2)
        tabsB = ph.enter_context(nc.sbuf_tensor("tabsB", [128, NWT, 2, 64], F32))
        tabsB_res = Res()
        build_tabs(tabsB, tabsB_res, pos_winT, NWT, None, "B")
        W = ph.enter_context(nc.sbuf_tensor("W", [128, 8, 2560], BF16))
        Wres = Res()
        hTall = ph.enter_context(nc.sbuf_tensor("hTall", [128, 8, NQ], BF16))
        hTall_res = Res()
        load_w(W, Wres, w_in, 0, 1024, 0, 2560)

        def tileB(t):
            own = t >= 15
            isq = t >= 16
            qt = t - 16
            hT, hTr = yield from frontend(xwin, t)
            yield
            proj_v(W, Wres, hT, hTr, 1920, 384, Vd_d, t * 128)
            if isq:
                P.dma("sp", (lambda qt=qt, hT=hT: lambda e: e.dma_start(out=hTall[:, :, qt * 128:(qt + 1) * 128], in_=hT[:]))(),
                      reads=[hTr], writes=[hTall_res])
            groups = [(1536, 6, 3, 1.0, KdT_d, t * 128, True)]
            if own:
                groups.append((384, 6, 1, 1.0, KaTo_d, (t - 15) * 128, True))
            if isq:
                groups += [(0, 6, 0, 0.125, QaT_d, qt * 128, True), (1152, 6, 2, 0.125, QdT_d, qt * 128, True),
                           (2304, 4, 4, 0.125, QmT_d, qt * 128, False)]
            first = True
            for (c0, nh, gidx, scl, dst, dcol, rope) in groups:
                pfk = proj_evac(W, Wres, hT, hTr, c0, nh * 64)
                if first and own:
                    proj_v(W, Wres, hT, hTr, 768, 384, Vao_d, (t - 15) * 128)
                first = False
                yield
                ob, obr = yield from head_post(pfk, nh, tabsB if rope else None, tabsB_res if rope else None, t, scl, gidx,
                                               rope=rope, split=True)
                yield
                to_T_dram(ob, obr, nh, dst, dcol)
        run_pipeline((tileB(t) for t in range(NWT)), MAXD)

        plan_barrier(P)
        groups = [(0, 512), (512, 512), (1024, 512), (1536, 512), (2048, 128)]
        gw = [Res(), Res()]

        def load_gw(part):
            hb = part % 2
            for kf in range(8):
                P.dma("pool", (lambda kf=kf, hb=hb, part=part: lambda e: e.dma_start(
                    out=W[:, kf, hb * 1024:(hb + 1) * 1024],
                    in_=w_in[kf * 128:(kf + 1) * 128, 2560 + part * 1024:2560 + (part + 1) * 1024]))(), writes=[gw[hb]])
        load_gw(0)
        load_gw(1)
        gi_ = 0
        for cc in range(24):
            part, cl = cc // 8, cc % 8
            hb = part % 2
            for (g0, gn) in groups:
                bank = 2 + (gi_ % 6)
                gi_ += 1
                for kf in range(8):
                    P.op("pe", (lambda kf=kf, hb=hb, cl=cl, g0=g0, gn=gn, bank=bank: lambda e: e.matmul(
                        ps[bank][:, 0:gn], lhsT=W[:, kf, hb * 1024 + cl * 128:hb * 1024 + (cl + 1) * 128], rhs=hTall[:, kf, g0:g0 + gn],
                        start=(kf == 0), stop=(kf == 7)))(), reads=[gw[hb], hTall_res], writes=[psr[bank]])
                gs_, gsr = RG["gsb"].next()
                P.op("act", (lambda gs_=gs_, gn=gn, bank=bank: lambda e: e.activation(out=gs_[:, 0:gn], in_=ps[bank][:, 0:gn], func=AF.Sigmoid))(),
                     reads=[psr[bank]], writes=[gsr])
                P.dma("act", (lambda gs_=gs_, cc=cc, g0=g0, gn=gn: lambda e: e.dma_start(out=sig_d[cc, :, g0:g0 + gn], in_=gs_[:, 0:gn]))(),
                      reads=[gsr])
            if cc == 7:
                load_gw(2)
        plan_barrier(P)

        gmem = ph.enter_context(nc.sbuf_tensor("gmem", [128, D], F32))
        P.dma("sp", lambda e: e.dma_start(out=gmem[:], in_=g_mem), writes=[cres])
        load_w(W, Wres, w_mkv, 0, 1024, 0, 512)

        def tileC(t):
            hT, hTr = yield from frontend(mem, t, gtile=gmem)
            b1 = nb()
            proj(W, Wres, hT, hTr, 0, 512, b1)
            pf, pfr = RG["pf"].next()
            P.op("act", lambda e: e.copy(out=pf[:, 0:256], in_=ps[b1][:, 0:256]), reads=[psr[b1]], writes=[pfr])
            v_to_dram(b1, Mv_d, t * 128, 256, 256)
            yield
            ob, obr = yield from head_post((pf, pfr), 4, None, None, t, 1.0, 5, rope=False)
            yield
            to_T_dram(ob, obr, 4, MkT_d, t * 128)
        run_pipeline((tileC(t) for t in range(2)), MAXD)

    dres = Res()
    plan_barrier(P)

    O_BANK0 = 4

    def oacc(qt):
        b = O_BANK0 + qt // 7
        c0 = (qt % 7) * 65
        return ps[b][:, c0:c0 + 65], psr[b]

    attnT = es.enter_context(nc.sbuf_tensor("attnT", [128, 8, NQ], BF16))
    attnT_res = Res()
    with ExitStack() as at:
        attn = at.enter_context(nc.sbuf_tensor("attn", [128, NQT, 1024], BF16))
        attn_res = Res()
        pt_r = ring("pt", [128, 1024], BF16, 4, at)
        den_r = ring("den", [128, 2], F32, 4, at)
        tri2 = at.enter_context(nc.sbuf_tensor("tri2", [128, 256], BF16))
        P.op("dve", lambda e: e.memset(tri2[:, 0:128], 1.0), writes=[cres])
        P.dma("sp", lambda e: e.dma_start(out=tri2[:, 128:256], in_=c_tri), writes=[cres])

        sst = {"i": 0, "pend": []}
        LAG = 2

        def flush(keep):
            while len(sst["pend"]) > keep:
                for args in sst["pend"].pop(0):
                    P.op(*args[0], **args[1])

        sdr = [Res(), Res()]

        def step(qk, n, pv, mask=None, mask_res=None):
            b = sst["i"] % 2
            sst["i"] += 1
            for (c0, w, lhsT, rhs, rres) in qk:
                P.op("pe", (lambda c0=c0, w=w, lhsT=lhsT, rhs=rhs, b=b: lambda e: e.matmul(
                    psd[b][:, c0:c0 + w], lhsT=lhsT, rhs=rhs, start=True, stop=True))(), reads=rres, writes=[sdr[b]])
            pt, ptr = pt_r.next()
            P.op("act", (lambda pt=pt, b=b, n=n: lambda e: e.activation(out=pt[:, 0:n], in_=psd[b][:, 0:n], func=AF.Exp))(),
                 reads=[sdr[b]], writes=[ptr])
            if mask is not None:
                P.op("dve", (lambda pt=pt, n=n, mask=mask: lambda e: e.tensor_tensor(
                    out=pt[:, 0:n], in0=pt[:, 0:n], in1=mask, op=ALU.mult))(), reads=[ptr, mask_res], writes=[ptr])
            pvl = []
            for (c0, vap, vres, qt) in pv:
                o, ores = oacc(qt)
                pvl.append((("pe", (lambda o=o, pt=pt, c0=c0, vap=vap: lambda e: e.matmul(
                    o, lhsT=pt[:, c0:c0 + 128], rhs=vap, start=False, stop=False, skip_group_check=True))()),
                    dict(reads=[ptr, vres], writes=[ores])))
            sst["pend"].append(pvl)
            flush(LAG)

        def zero_acc():
            flush(0)
            for b in range(3):
                P.op("dve", (lambda b=b: lambda e: e.memset(ps[O_BANK0 + b][:], 0.0))(), writes=[psr[O_BANK0 + b]])

        def finalize(col0):
            flush(0)
            for qt in range(NQT):
                o, ores = oacc(qt)
                dn, dnr = den_r.next()
                P.op("dve", (lambda o=o, dn=dn: lambda e: e.tensor_scalar(out=dn[:, 0:1], in0=o[:, 64:65], scalar1=1e-30,
                                                                         scalar2=None, op0=ALU.max))(), reads=[ores], writes=[dnr])
                P.op("dve", (lambda dn=dn: lambda e: e.reciprocal(out=dn[:, 1:2], in_=dn[:, 0:1]))(), reads=[dnr], writes=[dnr])
                P.op("dve", (lambda o=o, dn=dn, qt=qt: lambda e: e.tensor_scalar(
                    out=attn[:, qt, col0:col0 + 64], in0=o[:, 0:64], scalar1=dn[:, 1:2], scalar2=None, op0=ALU.mult))(),
                     reads=[ores, dnr], writes=[attn_res])

        with ExitStack() as mo:
            kt = [mo.enter_context(nc.sbuf_tensor(f"kt{i}", [128, S], BF16)) for i in range(2)]
            ktr = [Res(), Res()]
            va = [mo.enter_context(nc.sbuf_tensor(f"vaug{i}", [128, 128, 65], BF16)) for i in range(2)]
            var = [Res(), Res()]
            qa_ = [mo.enter_context(nc.sbuf_tensor(f"qaug{i}", [128, NQ], BF16)) for i in range(2)]
            qar = [Res(), Res()]
            kto = [mo.enter_context(nc.sbuf_tensor(f"kto{i}", [128, NO], BF16)) for i in range(2)]
            ktor = [Res(), Res()]
            vo = [mo.enter_context(nc.sbuf_tensor(f"vo{i}", [128, NOT, 65], BF16)) for i in range(2)]
            vor = [Res(), Res()]
            gmask = mo.enter_context(nc.sbuf_tensor("gmask", [128, NQT, 64], F32))
            P.dma("sp", lambda e: e.dma_start(out=gmask[:], in_=c_gmask), writes=[cres])
            km32 = mo.enter_context(nc.sbuf_tensor("km32", [128, 64], F32))
            kmT = mo.enter_context(nc.sbuf_tensor("kmT", [128, 64], BF16))
            kmr = Res()
            g32_r = ring("g32", [128, 64], F32, 2, mo)
            t8_r = ring("t8", [128, 16], F32, 2, mo)
            bq_r = ring("bq", [128, 128], BF16, 2, mo)
            for i in range(2):
                P.dma("act" if i else "sp", (lambda i=i: lambda e: e.dma_start(out=kt[i][64:128, :], in_=c_tmat))(), writes=[ktr[i]])
                P.op("pool", (lambda i=i: lambda e: e.memset(va[i][:, :, 64:65], 1.0))(), writes=[var[i]])
                P.op("pool", (lambda i=i: lambda e: e.memset(vo[i][:, :, 64:65], 1.0))(), writes=[vor[i]])
                P.op("pool", (lambda i=i: lambda e: e.memset(kto[i][64:128, :], 0.0))(), writes=[ktor[i]])
                bq, bqr = bq_r.next()
                P.op("pool", (lambda bq=bq: lambda e: e.memset(bq[:, 0:64], 0.0))(), writes=[bqr])

            def moba_load(h):
                i = h % 2
                P.dma("sp", (lambda i=i, h=h: lambda e: e.dma_start(out=qa_[i][0:64, :], in_=QaT_d[h]))(), writes=[qar[i]])
                for qq in range(4):
                    P.dma("sp" if qq % 2 == 0 else "act", (lambda qq=qq, i=i, h=h: lambda e: e.dma_start(
                        out=kt[i][0:64, qq * 4096:(qq + 1) * 4096], in_=KaT_d[h, :, qq * 4096:(qq + 1) * 4096]))(),
                        writes=[ktr[i]])
                P.dma("act", (lambda i=i, h=h: lambda e: e.dma_start(out=kto[i][0:64, :], in_=KaTo_d[h]))(), writes=[ktor[i]])
                vsrc = Va_d.rearrange("(ch p) c -> p ch c", p=128)
                for qq in range(16):
                    P.dma("sp" if qq % 2 == 0 else "act", (lambda qq=qq, i=i, h=h: lambda e: e.dma_start(
                        out=va[i][:, qq * 8:(qq + 1) * 8, 0:64], in_=vsrc[:, qq * 8:(qq + 1) * 8, h * 64:(h + 1) * 64]))(),
                        writes=[var[i]])
                for qq in range(3):
                    P.dma("sp", (lambda i=i, h=h, qq=qq: lambda e: e.dma_start(
                        out=vo[i][:, qq * 6:(qq + 1) * 6, 0:64],
                        in_=Vao_d.rearrange("(ch p) c -> p ch c", p=128)[:, qq * 6:(qq + 1) * 6, h * 64:(h + 1) * 64]))(),
                        writes=[vor[i]])

            def gate_gen(h):
                i = h % 2
                KT, Q = kt[i], qa_[i]
                P.op("dve", (lambda KT=KT: lambda e: e.tensor_reduce(
                    out=km32[0:64, :], in_=KT[0:64, :].rearrange("p (n k) -> p n k", k=256), axis=AX.X, op=ALU.add))(),
                     reads=[ktr[i]], writes=[kmr])
                P.op("act", lambda e: e.copy(out=kmT[0:64, :], in_=km32[0:64, :]), reads=[kmr], writes=[kmr])
                yield
                for qt in range(NQT):
                    qs = slice(qt * 128, (qt + 1) * 128)
                    P.op("pe", (lambda Q=Q, qs=qs: lambda e: e.matmul(ps[7][:, 0:64], lhsT=Q[0:64, qs], rhs=kmT[0:64, :],
                                                                      start=True, stop=True))(),
                         reads=[qar[i], kmr], writes=[psr[7]])
                    g32, g32r = g32_r.next()
                    t8, t8r = t8_r.next()
                    bq, bqr = bq_r.next()
                    P.op("dve", (lambda g32=g32, qt=qt: lambda e: e.tensor_tensor(out=g32[:], in0=ps[7][:, 0:64], in1=gmask[:, qt, :],
                                                                                  op=ALU.add))(), reads=[psr[7], cres], writes=[g32r])
                    P.op("dve", (lambda g32=g32, t8=t8: lambda e: e.max(out=t8[:, 0:8], in_=g32[:]))(), reads=[g32r], writes=[t8r])
                    P.op("dve", (lambda t8=t8: lambda e: e.tensor_scalar(out=t8[:, 8:9], in0=t8[:, 2:3], scalar1=-1e8, scalar2=None,
                                                                         op0=ALU.max))(), reads=[t8r], writes=[t8r])
                    P.op("dve", (lambda g32=g32, t8=t8, bq=bq: lambda e: e.tensor_scalar(
                        out=bq[:, 64:128], in0=g32[:], scalar1=t8[:, 8:9], scalar2=1.0, op0=ALU.is_ge, op1=ALU.subtract))(),
                         reads=[g32r, t8r], writes=[bqr])
                    P.op("pe", (lambda bq=bq: lambda e: e.matmul(ps[7][:, 128:256], lhsT=bq[:], rhs=ident[:], start=True, stop=True))(),
                         reads=[bqr, cres], writes=[psr[7]])
                    P.op("act", (lambda Q=Q, qs=qs: lambda e: e.activation(out=Q[64:128, qs], in_=ps[7][64:128, 128:256],
                                                                           func=AF.Identity, scale=30000.0))(),
                         reads=[psr[7]], writes=[qar[i]])
                    yield

            moba_load(0)
            for _ in gate_gen(0):
                pass
            for h in range(6):
                i = h % 2
                gnext = None
                if h + 1 < 6:
                    flush(0)
                    moba_load(h + 1)
                    gnext = gate_gen(h + 1)
                KT, V, Q, KO, VO = kt[i], va[i], qa_[i], kto[i], vo[i]
                nstep = 0
                zero_acc()
                for ch in range(128):
                    j = ch // 2
                    for (p0, n) in ((0, 1024), (1024, 1024), (2048, 128)):
                        plist = []
                        for q0 in range(p0, p0 + n, 512):
                            pi_ = q0 // 512
                            w = min(512, NQ - q0)
                            if (pi_ < 4 and j >= 57 + 2 * pi_) or (pi_ == 4 and j >= 63):
                                continue
                            plist.append((q0, w))
                        if not plist:
                            continue
                        base = plist[0][0]
                        qk = [(q0 - base, w, KT[:, ch * 128:(ch + 1) * 128], Q[:, q0:q0 + w], [ktr[i], qar[i]]) for (q0, w) in plist]
                        ntot = plist[-1][0] + plist[-1][1] - base
                        pv = [(q0 - base + k * 128, V[:, ch, :], var[i], q0 // 128 + k) for (q0, w) in plist for k in range(w // 128)]
                        step(qk, ntot, pv)
                        nstep += 1
                        if gnext is not None and nstep >= 160 and nstep % 10 == 0:
                            next(gnext, None)
                for qt in range(NQT):
                    o = 1 + qt
                    qs = slice(qt * 128, (qt + 1) * 128)
                    if qt % 2 == 1:
                        qk = [(0, 128, KO[:, o * 128:(o + 1) * 128], Q[:, qs], [ktor[i], qar[i]])]
                        pv = [(0, VO[:, o, :], vor[i], qt)]
                        step(qk, 128, pv, mask=tri2[:, 128:256], mask_res=cres)
                    else:
                        qk = [(0, 128, KO[:, (o - 1) * 128:o * 128], Q[:, qs], [ktor[i], qar[i]]),
                              (128, 128, KO[:, o * 128:(o + 1) * 128], Q[:, qs], [ktor[i], qar[i]])]
                        pv = [(0, VO[:, o - 1, :], vor[i], qt), (128, VO[:, o, :], vor[i], qt)]
                        step(qk, 256, pv, mask=tri2[:, 0:256], mask_res=cres)
                if gnext is not None:
                    for _ in gnext:
                        pass
                finalize(h * 64)

        flush(0)
        plan_barrier(P)
        with ExitStack() as di:
            kd_ = [di.enter_context(nc.sbuf_tensor(f"kds{i}", [128, NW], BF16)) for i in range(2)]
            kdr = [Res(), Res()]
            qd_ = [di.enter_context(nc.sbuf_tensor(f"qds{i}", [128, NQ], BF16)) for i in range(2)]
            qdr = [Res(), Res()]
            vd_ = [di.enter_context(nc.sbuf_tensor(f"vds{i}", [128, NWT, 65], BF16)) for i in range(2)]
            vdr = [Res(), Res()]
            mdil = di.enter_context(nc.sbuf_tensor("mdil", [128, 17, 128], BF16))
            kvalid = di.enter_context(nc.sbuf_tensor("kvalid", [128, NWT], F32))
            P.dma("sp", lambda e: e.dma_start(out=mdil[:], in_=c_mdil), writes=[cres])
            P.dma("sp", lambda e: e.dma_start(out=kvalid[:], in_=c_kvalid), writes=[cres])

            def dil_load(h):
                i = h % 2
                P.dma("sp", (lambda i=i, h=h: lambda e: e.dma_start(out=kd_[i][0:64, :], in_=KdT_d[h]))(), writes=[kdr[i]])
                P.dma("act", (lambda i=i, h=h: lambda e: e.dma_start(out=qd_[i][0:64, :], in_=QdT_d[h]))(), writes=[qdr[i]])
                P.op("pool", (lambda i=i: lambda e: e.memset(vd_[i][:, :, 64:65], 1.0))(), writes=[vdr[i]])
                for c0_, c1_ in ((0, 8), (8, 16), (16, 24), (24, 33)):
                    P.dma("sp", (lambda i=i, h=h, c0_=c0_, c1_=c1_: lambda e: e.dma_start(
                        out=vd_[i][:, c0_:c1_, 0:64], in_=Vd_d.rearrange("(ch p) c -> p ch c", p=128)[:, c0_:c1_, h * 64:(h + 1) * 64]))(),
                        writes=[vdr[i]])
                P.op("dve", (lambda i=i: lambda e: e.tensor_tensor(
                    out=vd_[i][:], in0=vd_[i][:], in1=kvalid[:].unsqueeze(2).to_broadcast([128, NWT, 65]), op=ALU.mult))(),
                     reads=[vdr[i], cres], writes=[vdr[i]])

            dil_load(0)
            for h in range(6):
                i = h % 2
                if h + 1 < 6:
                    flush(0)
                    dil_load(h + 1)
                zero_acc()
                for qt in range(NQT):
                    w = 16 + qt
                    qs = slice(qt * 128, (qt + 1) * 128)
                    for d0 in (0, 4, 8, 12, 16):
                        cnt = 4 if d0 < 16 else 1
                        qk = [(k * 128, 128, kd_[i][0:64, (w - d0 - k) * 128:(w - d0 - k + 1) * 128], qd_[i][0:64, qs],
                               [kdr[i], qdr[i]]) for k in range(cnt)]
                        pv = [(k * 128, vd_[i][:, w - d0 - k, :], vdr[i], qt) for k in range(cnt)]
                        step(qk, cnt * 128, pv, mask=mdil[:, d0:d0 + cnt, :].rearrange("p a b -> p (a b)"), mask_res=cres)
                finalize(384 + h * 64)

        flush(0)
        plan_barrier(P)
        with ExitStack() as me:
            mk_ = [me.enter_context(nc.sbuf_tensor(f"mks{i}", [128, 256], BF16)) for i in range(2)]
            mkr = [Res(), Res()]
            qm_ = [me.enter_context(nc.sbuf_tensor(f"qms{i}", [128, NQ], BF16)) for i in range(2)]
            qmr = [Res(), Res()]
            mv_ = me.enter_context(nc.sbuf_tensor("mvs", [128, 2, 4, 65], BF16))
            mvr = Res()
            P.op("pool", lambda e: e.memset(mv_[:].rearrange("p a h c -> p (a h) c")[:, :, 64:65], 1.0), writes=[mvr])
            for chh in range(2):
                P.dma("sp", (lambda chh=chh: lambda e: e.dma_start(
                    out=mv_[:, chh, :, 0:64], in_=Mv_d[chh * 128:(chh + 1) * 128, :].rearrange("p (h d) -> p h d", d=64)))(),
                    writes=[mvr])
            for h in range(4):
                i = h % 2
                flush(0)
                P.dma("sp", (lambda i=i, h=h: lambda e: e.dma_start(out=mk_[i][0:64, :], in_=MkT_d[h]))(), writes=[mkr[i]])
                P.dma("act", (lambda i=i, h=h: lambda e: e.dma_start(out=qm_[i][0:64, :], in_=QmT_d[h]))(), writes=[qmr[i]])
                zero_acc()
                for qt in range(NQT):
                    qs = slice(qt * 128, (qt + 1) * 128)
                    qk = [(k * 128, 128, mk_[i][0:64, k * 128:(k + 1) * 128], qm_[i][0:64, qs], [mkr[i], qmr[i]]) for k in range(2)]
                    pv = [(k * 128, mv_[:, k, h, :], mvr, qt) for k in range(2)]
                    step(qk, 256, pv)
                finalize(768 + h * 64)
        flush(0)
        plan_barrier(P)

        if "attn_dbg" in DEBUG_OUT:
            attn_dbg = dscr("attn_dbg", [NQ, 1024])
            P.dma("sp", lambda e: e.dma_start(out=attn_dbg.rearrange("(t p) c -> p t c", p=128), in_=attn[:]), reads=[attn_res])
        for qt in range(NQT):
            for a in range(2):
                b = (2 * qt + a) % 3
                for k in range(4):
                    fc = a * 4 + k
                    P.op("pe", (lambda qt=qt, fc=fc, b=b, k=k: lambda e: e.matmul(
                        ps[b][:, k * 128:(k + 1) * 128], lhsT=attn[:, qt, fc * 128:(fc + 1) * 128], rhs=ident[:],
                        start=True, stop=True))(), reads=[attn_res, cres], writes=[psr[b]])
                P.op("act" if a == 0 else "dve", (lambda qt=qt, a=a, b=b: (
                    (lambda e: e.copy(out=attnT[:, a * 4:(a + 1) * 4, qt * 128:(qt + 1) * 128],
                                      in_=ps[b][:].rearrange("p (a b) -> p a b", b=128))) if a == 0 else
                    (lambda e: e.tensor_copy(out=attnT[:, a * 4:(a + 1) * 4, qt * 128:(qt + 1) * 128],
                                             in_=ps[b][:].rearrange("p (a b) -> p a b", b=128)))))(),
                     reads=[psr[b]], writes=[attnT_res])

    plan_barrier(P)
    h2T = es.enter_context(nc.sbuf_tensor("h2T", [128, 8, NQ], BF16))
    h2T_res = Res()
    with ExitStack() as mg:
        wb = mg.enter_context(nc.sbuf_tensor("wb", [128, 8, D], BF16))
        wbr = Res()
        load_w(wb, wbr, w_ba, 0, 384, 0, D)
        for kf in range(3):
            P.dma("pool", (lambda kf=kf: lambda e: e.dma_start(out=wb[:, 3 + kf, :], in_=w_bd[kf * 128:(kf + 1) * 128, :]))(), writes=[wbr])
        for kf in range(2):
            P.dma("pool", (lambda kf=kf: lambda e: e.dma_start(out=wb[:, 6 + kf, :], in_=w_bm[kf * 128:(kf + 1) * 128, :]))(), writes=[wbr])
        wo = mg.enter_context(nc.sbuf_tensor("wo", [128, 8, D], BF16))
        wor = Res()
        load_w(wo, wor, w_out, 0, 1024, 0, D)
        mergedT = mg.enter_context(nc.sbuf_tensor("mergedT", [128, 8, NQ], BF16))
        mres = Res()
        m1 = mg.enter_context(ExitStack())
        sg_r = ring("sg", [128, 3, NQ], BF16, 2, m1)
        t1_r = ring("t1", [128, 512], F32, 2, m1)
        t2_r = ring("t2", [128, 512], F32, 2, m1)
        groups = [(0, 512), (512, 512), (1024, 512), (1536, 512), (2048, 128)]
        cnt_ = 0
        for fc in range(8):
            sg, sgr = sg_r.next()
            for b3 in range(3):
                P.dma("sp" if b3 != 1 else "act", (lambda sg=sg, b3=b3, fc=fc: lambda e: e.dma_start(
                    out=sg[:, b3, :], in_=sig_d[b3 * 8 + fc]))(), writes=[sgr])
            for (g0, gn) in groups:
                bb = (cnt_ % 2) * 3
                cnt_ += 1
                for b3, (k0, nk) in enumerate(((0, 3), (3, 3), (6, 2))):
                    for k in range(nk):
                        P.op("pe", (lambda bb=bb, b3=b3, k0=k0, k=k, nk=nk, fc=fc, g0=g0, gn=gn: lambda e: e.matmul(
                            ps[bb + b3][:, 0:gn], lhsT=wb[:, k0 + k, fc * 128:(fc + 1) * 128], rhs=attnT[:, k0 + k, g0:g0 + gn],
                            start=(k == 0), stop=(k == nk - 1)))(), reads=[wbr, attnT_res], writes=[psr[bb + b3]])
                t1, t1r = t1_r.next()
                t2, t2r = t2_r.next()
                P.op("dve", (lambda t1=t1, bb=bb, sg=sg, g0=g0, gn=gn: lambda e: e.tensor_tensor(
                    out=t1[:, 0:gn], in0=ps[bb][:, 0:gn], in1=sg[:, 0, g0:g0 + gn], op=ALU.mult))(),
                     reads=[psr[bb], sgr], writes=[t1r])
                P.op("dve", (lambda t2=t2, bb=bb, sg=sg, g0=g0, gn=gn: lambda e: e.tensor_tensor(
                    out=t2[:, 0:gn], in0=ps[bb + 1][:, 0:gn], in1=sg[:, 1, g0:g0 + gn], op=ALU.mult))(),
                     reads=[psr[bb + 1], sgr], writes=[t2r])
                P.op("dve", (lambda t1=t1, t2=t2, gn=gn: lambda e: e.tensor_tensor(
                    out=t1[:, 0:gn], in0=t1[:, 0:gn], in1=t2[:, 0:gn], op=ALU.add))(), reads=[t1r, t2r], writes=[t1r])
                P.op("dve", (lambda t2=t2, bb=bb, sg=sg, g0=g0, gn=gn: lambda e: e.tensor_tensor(
                    out=t2[:, 0:gn], in0=ps[bb + 2][:, 0:gn], in1=sg[:, 2, g0:g0 + gn], op=ALU.mult))(),
                     reads=[psr[bb + 2], sgr, t1r], writes=[t2r])
                P.op("dve", (lambda t1=t1, t2=t2, fc=fc, g0=g0, gn=gn: lambda e: e.tensor_tensor(
                    out=mergedT[:, fc, g0:g0 + gn], in0=t1[:, 0:gn], in1=t2[:, 0:gn], op=ALU.add))(),
                     reads=[t1r, t2r], writes=[mres])

        m1.close()
        plan_barrier(P)
        gffn = mg.enter_context(nc.sbuf_tensor("gffn", [128, D], F32))
        P.dma("sp", lambda e: e.dma_start(out=gffn[:], in_=g_ffn), writes=[cres])
        xq_r = ring("xq", [128, D], F32, 3, mg)
        x1_r = ring("x1t", [128, D], F32, 5, mg)
        jk2_r = ring("jk2", [128, D], BF16, 1, mg)
        ss2_r = ring("ss2", [128, 1], F32, 5, mg)
        xn2_r = ring("xn2", [128, D], BF16, 2, mg)
        obk = [0]

        def tileO(qt):
            xq, xqr = xq_r.next()
            P.dma("sp", (lambda xq=xq, qt=qt: lambda e: e.dma_start(out=xq[:], in_=xwin[(16 + qt) * 128:(17 + qt) * 128, :]))(),
                  writes=[xqr])
            x1t, x1r = x1_r.next()
            for half in range(2):
                b = 2 + obk[0] % 6
                obk[0] += 1
                for fc in range(8):
                    P.op("pe", (lambda b=b, fc=fc, qt=qt, half=half: lambda e: e.matmul(
                        ps[b][:], lhsT=mergedT[:, fc, qt * 128:(qt + 1) * 128], rhs=wo[:, fc, half * 512:(half + 1) * 512],
                        start=(fc == 0), stop=(fc == 7)))(), reads=[mres, wor], writes=[psr[b]])
                P.op("dve", (lambda b=b, x1t=x1t, xq=xq, half=half: lambda e: e.tensor_tensor(
                    out=x1t[:, half * 512:(half + 1) * 512], in0=ps[b][:], in1=xq[:, half * 512:(half + 1) * 512], op=ALU.add))(),
                     reads=[psr[b], xqr], writes=[x1r])
            yield
            P.dma("act", (lambda x1t=x1t, qt=qt: lambda e: e.dma_start(out=x1_d[qt * 128:(qt + 1) * 128, :], in_=x1t[:]))(),
                  reads=[x1r])
            jk, jkr = jk2_r.next()
            ss, ssr = ss2_r.next()
            P.op("act", (lambda jk=jk, x1t=x1t, ss=ss: lambda e: e.activation(out=jk[:], in_=x1t[:], func=AF.Square, accum_out=ss[:]))(),
                 reads=[x1r], writes=[jkr, ssr])
            P.op("act", lambda e: e.copy(out=fence[:, 0:1], in_=fence[:, 1:2]), writes=[ssr])
            P.op("act", (lambda ss=ss: lambda e: e.activation(out=ss[:], in_=ss[:], func=AF.Sqrt, scale=1.0 / D, bias=epsc[:, 0:1]))(),
                 reads=[ssr, cres], writes=[ssr])
            P.op("dve", (lambda ss=ss: lambda e: e.reciprocal(out=ss[:], in_=ss[:]))(), reads=[ssr], writes=[ssr])
            yield
            xn, xnr = xn2_r.next()
            P.op("dve", (lambda xn=xn, x1t=x1t, ss=ss: lambda e: e.scalar_tensor_tensor(
                out=xn[:], in0=x1t[:], scalar=ss[:, 0:1], in1=gffn[:], op0=ALU.mult, op1=ALU.mult))(),
                 reads=[x1r, ssr, cres], writes=[xnr])
            for a in range(2):
                b = a
                for k in range(4):
                    fc = a * 4 + k
                    P.op("pe", (lambda xn=xn, fc=fc, b=b, k=k: lambda e: e.matmul(
                        ps[b][:, k * 128:(k + 1) * 128], lhsT=xn[:, fc * 128:(fc + 1) * 128], rhs=ident[:],
                        start=True, stop=True))(), reads=[xnr, cres], writes=[psr[b]])
                if a == 0:
                    P.op("act", (lambda qt=qt, b=b: lambda e: e.copy(
                        out=h2T[:, 0:4, qt * 128:(qt + 1) * 128], in_=ps[b][:].rearrange("p (a b) -> p a b", b=128)))(),
                         reads=[psr[b]], writes=[h2T_res])
                else:
                    P.op("dve", (lambda qt=qt, b=b: lambda e: e.tensor_copy(
                        out=h2T[:, 4:8, qt * 128:(qt + 1) * 128], in_=ps[b][:].rearrange("p (a b) -> p a b", b=128)))(),
                         reads=[psr[b]], writes=[h2T_res])
        run_pipeline((tileO(qt) for qt in range(NQT)), 3)

    plan_barrier(P)
    actT = es.enter_context(nc.sbuf_tensor("actT", [128, 22, TOK], BF16))
    actT_res = Res()
    cws = es.enter_context(nc.sbuf_tensor("cws", [128, 44, 4], F32))
    hval = es.enter_context(nc.sbuf_tensor("hval", [128, 1], F32))
    P.dma("sp", lambda e: e.dma_start(out=cws[:], in_=cw), writes=[cres])
    P.dma("sp", lambda e: e.dma_start(out=hval[:], in_=c_hvalid), writes=[cres])
    with ExitStack() as up:
        wu_r = ring("wu", [128, 8, 256], BF16, 3, up)
        ext_r = [ring("extg", [128, 514], F32, 3, up), ring("extv", [128, 514], F32, 3, up)]
        a_r = [ring("ag", [128, 512], F32, 3, up), ring("av", [128, 512], F32, 3, up)]
        s_r = ring("sil", [128, 512], F32, 2, up)
        wsrc = w_up.rearrange("(kf p) c -> p kf c", p=128)
        bi = [0]
        wus = {}

        def stepU(k, tg):
            if tg == 0:
                wu, wur = wu_r.next()
                wus[k] = (wu, wur)
                for part in range(2):
                    c0 = part * DFF + k * 128
                    P.dma("pool", (lambda wu=wu, part=part, c0=c0: lambda e: e.dma_start(
                        out=wu[:, :, part * 128:(part + 1) * 128], in_=wsrc[:, :, c0:c0 + 128]))(), writes=[wur])
            wu, wur = wus[k]
            exts = []
            for part in range(2):
                ext, extr = ext_r[part].next()
                hc0 = 126 + tg * 512
                b = 2 + bi[0] % 6
                bi[0] += 1
                for kf in range(8):
                    P.op("pe", (lambda b=b, wu=wu, part=part, kf=kf, hc0=hc0: lambda e: e.matmul(
                        ps[b][:, 0:2], lhsT=wu[:, kf, part * 128:(part + 1) * 128], rhs=h2T[:, kf, hc0:hc0 + 2],
                        start=(kf == 0), stop=(kf == 7)))(), reads=[wur, h2T_res], writes=[psr[b]])
                if tg == 0:
                    P.op("act", (lambda b=b, ext=ext: lambda e: e.activation(out=ext[:, 0:2], in_=ps[b][:, 0:2], func=AF.Identity,
                                                                             scale=hval[:, 0:1]))(),
                         reads=[psr[b], cres], writes=[extr])
                else:
                    P.op("act", (lambda b=b, ext=ext: lambda e: e.copy(out=ext[:, 0:2], in_=ps[b][:, 0:2]))(),
                         reads=[psr[b]], writes=[extr])
                b = 2 + bi[0] % 6
                bi[0] += 1
                for kf in range(8):
                    P.op("pe", (lambda b=b, wu=wu, part=part, kf=kf, tg=tg: lambda e: e.matmul(
                        ps[b][:], lhsT=wu[:, kf, part * 128:(part + 1) * 128],
                        rhs=h2T[:, kf, 128 + tg * 512:128 + (tg + 1) * 512], start=(kf == 0), stop=(kf == 7)))(),
                         reads=[wur, h2T_res], writes=[psr[b]])
                P.op("act", (lambda b=b, ext=ext: lambda e: e.copy(out=ext[:, 2:514], in_=ps[b][:]))(),
                     reads=[psr[b]], writes=[extr])
                ch = part * 22 + k
                a_, ar_ = a_r[part].next()
                P.op("act", (lambda b=b, a_=a_, ch=ch: lambda e: e.activation(out=a_[:], in_=ps[b][:], func=AF.Identity,
                                                                             scale=cws[:, ch, 2:3], bias=cws[:, ch, 3:4]))(),
                     reads=[psr[b], cres], writes=[ar_])
                exts.append((ext, extr, a_, ar_))
            yield
            outs = []
            for part in range(2):
                ext, extr, a_, ar_ = exts[part]
                ch = part * 22 + k
                P.op("dve", (lambda a_=a_, ext=ext, ch=ch: lambda e: e.scalar_tensor_tensor(
                    out=a_[:], in0=ext[:, 1:513], scalar=cws[:, ch, 1:2], in1=a_[:], op0=ALU.mult, op1=ALU.add))(),
                     reads=[extr, cres, ar_], writes=[ar_])
                P.op("dve", (lambda a_=a_, ext=ext, ch=ch: lambda e: e.scalar_tensor_tensor(
                    out=a_[:], in0=ext[:, 0:512], scalar=cws[:, ch, 0:1], in1=a_[:], op0=ALU.mult, op1=ALU.add))(),
                     reads=[extr, cres, ar_], writes=[ar_])
                outs.append((a_, ar_))
            yield
            sl, slr = s_r.next()
            (ag, agr), (av, avr) = outs
            P.op("act", (lambda sl=sl, ag=ag: lambda e: e.activation(out=sl[:], in_=ag[:], func=AF.Silu))(), reads=[agr], writes=[slr])
            P.op("dve", (lambda sl=sl, av=av, k=k, tg=tg: lambda e: e.tensor_tensor(
                out=actT[:, k, tg * 512:(tg + 1) * 512], in0=sl[:], in1=av[:], op=ALU.mult))(),
                 reads=[slr, avr], writes=[actT_res])
        run_pipeline((stepU(k, tg) for k in range(22) for tg in range(4)), 3)

    plan_barrier(P)
    with ExitStack() as dn:
        wdA = attnT[:].rearrange("p a b -> p (a b)").rearrange("p (k c) -> p k c", c=D)
        wdB = h2T[:].rearrange("p a b -> p (a b)")[:, 0:5 * D].rearrange("p (k c) -> p k c", c=D)

        def wdk(k):
            return (wdA[:, k, :], attnT_res) if k < 17 else (wdB[:, k - 17, :], h2T_res)
        for k in range(22):
            dst_, res_ = wdk(k)
            P.dma("pool", (lambda dst_=dst_, k=k: lambda e: e.dma_start(out=dst_, in_=w_dn[k * 128:(k + 1) * 128, :]))(),
                  writes=[res_])
        x1b_r = ring("x1b", [128, D], F32, 4, dn)
        yo_r = ring("yo", [128, D], F32, 2, dn)
        dbk = [0]

        def tileD(t):
            x1b, x1br = x1b_r.next()
            P.dma("sp", (lambda x1b=x1b, t=t: lambda e: e.dma_start(out=x1b[:], in_=x1_d[(1 + t) * 128:(2 + t) * 128, :]))(),
                  writes=[x1br])
            yield
            yo, yor = yo_r.next()
            for half in range(2):
                b = dbk[0] % 8
                dbk[0] += 1
                for k in range(22):
                    wk_, wkr_ = wdk(k)
                    P.op("pe", (lambda b=b, k=k, t=t, half=half, wk_=wk_: lambda e: e.matmul(
                        ps[b][:], lhsT=actT[:, k, t * 128:(t + 1) * 128], rhs=wk_[:, half * 512:(half + 1) * 512],
                        start=(k == 0), stop=(k == 21)))(), reads=[actT_res, wkr_], writes=[psr[b]])
                P.op("dve" if half == 0 else "act", (lambda b=b, yo=yo, x1b=x1b, half=half: (
                    (lambda e: e.tensor_tensor(out=yo[:, half * 512:(half + 1) * 512], in0=ps[b][:],
                                               in1=x1b[:, half * 512:(half + 1) * 512], op=ALU.add))))(),
                     reads=[psr[b], x1br], writes=[yor]) if half == 0 else \
                    P.op("dve", (lambda b=b, yo=yo, x1b=x1b, half=half: lambda e: e.tensor_tensor(
                        out=yo[:, half * 512:(half + 1) * 512], in0=ps[b][:], in1=x1b[:, half * 512:(half + 1) * 512], op=ALU.add))(),
                        reads=[psr[b], x1br], writes=[yor])
            P.dma("act", (lambda yo=yo, t=t: lambda e: e.dma_start(out=out_d[t * 128:(t + 1) * 128, :], in_=yo[:]))(), reads=[yor])
        run_pipeline((tileD(t) for t in range(16)), 3)

    SEM = {}
    for e_ in Plan.ENG:
        SEM[e_] = es.enter_context(nc.semaphore(f"s_{e_}"))
    for q in ("sp", "act", "pool"):
        for i in range(NDS):
            SEM[f"d_{q}_{i}"] = es.enter_context(nc.semaphore(f"d_{q}_{i}"))

    def replay(eng, e):
        for waits, fn, dkey in P.prog[eng]:
            for k, v in waits:
                e.wait_ge(SEM[k], v)
            if fn is None:
                continue
            ins = fn(e)
            if dkey is not None:
                ins.then_inc(SEM[dkey], 16)
            else:
                ins.then_inc(SEM[eng], 1)
        if eng in P.dq:
            for i in range(NDS):
                v = P.dq[eng]["val"][i]
                if v > 0:
                    e.wait_ge(SEM[f"d_{eng}_{i}"], v)

    with nc.Block() as block:
        @block.tensor
        def _(e):
            replay("pe", e)

        @block.scalar
        def _(e):
            replay("act", e)

        @block.vector
        def _(e):
            replay("dve", e)

        @block.gpsimd
        def _(e):
            replay("pool", e)

        @block.sync
        def _(e):
            replay("sp", e)
    es.close()
    return nc


def _host_consts(c):
    bf = ml_dtypes.bfloat16
    k = np.arange(128)[:, None]
    q = np.arange(128)[None, :]
    tri = (k <= q).astype(np.float32)
    mdil = np.zeros((128, 17, 128), np.float32)
    for dlt in range(17):
        dist = dlt * 128 + q - k
        m = ((dist >= 0) & (dist <= 128)).astype(np.float32)
        m += ((dist >= 0) & (dist <= 512) & (dist % 4 == 0))
        m += ((dist >= 0) & (dist <= 2048) & (dist % 16 == 0))
        mdil[:, dlt, :] = m
    tmat = (np.arange(S)[None, :] // 256 == np.arange(64)[:, None]).astype(np.float32)
    invf = (10000.0 ** (-np.arange(32, dtype=np.float32) / 32)).astype(np.float32)
    invf2 = np.broadcast_to(np.concatenate([invf, invf])[None], (128, 64))
    offs = np.broadcast_to(np.concatenate([np.zeros(32, np.float32), np.full(32, np.pi / 2, np.float32)])[None], (128, 64))
    gmask = np.zeros((128, NQT, 64), np.float32)
    for qt in range(NQT):
        tok0 = TOK * c - 128 + qt * 128
        own = tok0 // 256 if tok0 >= 0 else -1
        gmask[:, qt, :] = np.where(np.arange(64) < own, 0.0, -1e9)[None]
    kvalid = np.ones((128, NWT), np.float32)
    for wt in range(NWT):
        if TOK * c - 2176 + wt * 128 < 0:
            kvalid[:, wt] = 0.0
    hvalid = np.full((128, 1), 0.0 if c == 0 else 1.0, np.float32)
    return dict(c_ident=np.eye(128, dtype=np.float32).astype(bf), c_tri=tri.astype(bf), c_mdil=mdil.astype(bf),
                c_tmat=tmat.astype(bf), c_invf=np.ascontiguousarray(invf2), c_offs=np.ascontiguousarray(offs),
                c_gmask=gmask, c_kvalid=kvalid, c_hvalid=hvalid)


def kernel(x, mem, positions, mix_norm_g, mem_norm_g, w_in, moba_q_norm_g, moba_k_norm_g,
           dil_q_norm_g, dil_k_norm_g, mem_q_norm_g, mem_k_norm_g, w_mem_kv,
           w_branch_moba, w_branch_dil, w_branch_mem, w_out, ffn_norm_g,
           w_ffn_up, ffn_conv_w, ffn_conv_b, w_ffn_down):
    f = lambda a: np.ascontiguousarray(np.asarray(a, dtype=np.float32))
    x2 = f(x)[0]
    pos = np.ascontiguousarray(np.asarray(positions, dtype=np.int32)[0])
    bc = lambda v, n=128: np.ascontiguousarray(np.broadcast_to(f(v).reshape(1, -1), (n, f(v).size)))
    g6 = np.stack([bc(g[0]) for g in (moba_q_norm_g, moba_k_norm_g, dil_q_norm_g, dil_k_norm_g, mem_q_norm_g, mem_k_norm_g)], axis=1)
    cwt = np.concatenate([f(ffn_conv_w)[0], f(ffn_conv_b)[0][None]], axis=0)
    cwt = np.ascontiguousarray(cwt.reshape(4, 44, 128).transpose(2, 1, 0))
    shared = dict(xall=x2, pos_allT=np.ascontiguousarray(pos.reshape(S // 128, 128).T), mem=f(mem)[0], w_in=f(w_in)[0], w_mkv=f(w_mem_kv)[0],
                  w_ba=f(w_branch_moba)[0], w_bd=f(w_branch_dil)[0], w_bm=f(w_branch_mem)[0], w_out=f(w_out)[0],
                  w_up=f(w_ffn_up)[0], w_dn=f(w_ffn_down)[0], cw=cwt, g_mix=bc(mix_norm_g[0]), g_mem=bc(mem_norm_g[0]),
                  g_ffn=bc(ffn_norm_g[0]), g6=np.ascontiguousarray(g6))
    in_maps = []
    for c in range(NCORE):
        lo = TOK * c - 2176
        xw = np.zeros((NW, D), np.float32)
        pw = np.zeros((NW, 1), np.int32)
        a = max(lo, 0)
        xw[a - lo:] = x2[a:lo + NW]
        pw[a - lo:, 0] = pos[a:lo + NW]
        m = dict(shared)
        m.update(xwin=xw, pos_winT=np.ascontiguousarray(pw.reshape(NWT, 128).T))
        m.update(_host_consts(c))
        in_maps.append(m)
    nc = build_nc()
    res = run_bass_kernel_spmd(nc, in_maps, core_ids=list(range(NCORE)))
    out = np.concatenate([np.asarray(r["out"], dtype=np.float32) for r in res.results], axis=0)
    return out.reshape(1, S, D)
```

```python
import math
from contextlib import ExitStack

import ml_dtypes
import numpy as np

import concourse.bass as bass
import concourse.mybir as mybir
from concourse.bass_utils import run_bass_kernel_spmd

F32 = mybir.dt.float32
BF16 = mybir.dt.bfloat16
I32 = mybir.dt.int32
ALU = mybir.AluOpType
AF = mybir.ActivationFunctionType
AX = mybir.AxisListType

S = 16384
D = 1024
NCORE = 8
TOK = S // NCORE
NQT = 17
NQ = NQT * 128
NWT = 33
NW = NWT * 128
NOT = 18
NO = NOT * 128
DFF = 2816
EPS = 1e-6
NEG = -30000.0
NDS = 12
DEBUG_OUT = set()


class Res:
    __slots__ = ("w", "r")

    def __init__(self):
        self.w = None
        self.r = {}


class Plan:
    ENG = ("pe", "act", "dve", "pool", "sp")

    def __init__(self):
        self.prog = {e: [] for e in self.ENG}
        self.cnt = {e: 0 for e in self.ENG}
        self.seen = {e: {} for e in self.ENG}
        self.dq = {q: {"i": 0, "val": [0] * NDS} for q in ("sp", "act", "pool")}

    def _deps(self, eng, reads, writes):
        need = {}

        def add(tok):
            if tok is None:
                return
            k, v = tok
            if need.get(k, 0) < v:
                need[k] = v

        for r in reads:
            add(r.w)
        for w in writes:
            add(w.w)
            for t in w.r.values():
                add(t)
        out = []
        for k, v in need.items():
            if eng == "pe" and k == "pe":
                continue
            if self.seen[eng].get(k, 0) < v:
                self.seen[eng][k] = v
                out.append((k, v))
        return out

    def op(self, eng, fn, reads=(), writes=()):
        waits = self._deps(eng, reads, writes)
        self.cnt[eng] += 1
        tok = (eng, self.cnt[eng])
        self.prog[eng].append((waits, fn, None))
        for r in reads:
            r.r[eng] = tok
        for w in writes:
            w.w = tok
            w.r = {}

    def dma(self, q, fn, reads=(), writes=()):
        waits = self._deps(q, reads, writes)
        d = self.dq[q]
        i = d["i"] % NDS
        d["i"] += 1
        key = f"d_{q}_{i}"
        prev = d["val"][i]
        if prev > 0 and self.seen[q].get(key, 0) < prev:
            self.seen[q][key] = prev
            waits.append((key, prev))
        d["val"][i] = prev + 16
        tok = (key, prev + 16)
        self.prog[q].append((waits, fn, key))
        for r in reads:
            r.r[key] = tok
        for w in writes:
            w.w = tok
            w.r = {}


def plan_barrier(P):
    for e in P.ENG:
        waits = []
        for k2 in P.ENG:
            v = P.cnt[k2]
            if v > 0 and P.seen[e].get(k2, 0) < v:
                P.seen[e][k2] = v
                waits.append((k2, v))
        for q in P.dq:
            for i in range(NDS):
                v = P.dq[q]["val"][i]
                key = f"d_{q}_{i}"
                if v > 0 and P.seen[e].get(key, 0) < v:
                    P.seen[e][key] = v
                    waits.append((key, v))
        if waits:
            P.prog[e].append((waits, None, None))


def run_pipeline(gens, maxd=3):
    active = []
    for g in gens:
        while len(active) >= maxd:
            for a in list(active):
                if next(a, "done") == "done":
                    active.remove(a)
        for a in list(active):
            if next(a, "done") == "done":
                active.remove(a)
        active.append(g)
        if next(g, "done") == "done":
            active.remove(g)
    while active:
        for a in list(active):
            if next(a, "done") == "done":
                active.remove(a)


class Ring:
    def __init__(self, tiles):
        self.t = tiles
        self.res = [Res() for _ in tiles]
        self.i = 0

    def next(self):
        k = self.i % len(self.t)
        self.i += 1
        return self.t[k], self.res[k]


def build_nc():
    nc = bass.Bass("TRN2", target_bir_lowering=False)
    P = Plan()
    es = ExitStack()

    def din(name, shape, dt=F32):
        return nc.dram_tensor(name, list(shape), dt, kind="ExternalInput").ap()

    def dscr(name, shape, dt=BF16):
        if name in DEBUG_OUT:
            return nc.dram_tensor(name, list(shape), dt, kind="ExternalOutput").ap()
        return nc.dram_tensor(name, list(shape), dt).ap()

    xall = din("xall", [S, D])
    xwin = din("xwin", [NW, D])
    pos_allT = din("pos_allT", [128, S // 128], I32)
    pos_winT = din("pos_winT", [128, NWT], I32)
    mem = din("mem", [256, D])
    w_in = din("w_in", [D, 5632])
    w_mkv = din("w_mkv", [D, 512])
    w_ba = din("w_ba", [384, D])
    w_bd = din("w_bd", [384, D])
    w_bm = din("w_bm", [256, D])
    w_out = din("w_out", [D, D])
    w_up = din("w_up", [D, 5632])
    w_dn = din("w_dn", [DFF, D])
    cw = din("cw", [128, 44, 4])
    g_mix = din("g_mix", [128, D])
    g_mem = din("g_mem", [128, D])
    g_ffn = din("g_ffn", [128, D])
    g6 = din("g6", [128, 6, 64])
    c_ident = din("c_ident", [128, 128], BF16)
    c_tri = din("c_tri", [128, 128], BF16)
    c_mdil = din("c_mdil", [128, 17, 128], BF16)
    c_tmat = din("c_tmat", [64, S], BF16)
    c_invf = din("c_invf", [128, 64])
    c_offs = din("c_offs", [128, 64])
    c_gmask = din("c_gmask", [128, NQT, 64])
    c_kvalid = din("c_kvalid", [128, NWT])
    c_hvalid = din("c_hvalid", [128, 1])
    out_d = nc.dram_tensor("out", [TOK, D], F32, kind="ExternalOutput").ap()

    KaT_d = dscr("KaT_d", [6, 64, S])
    Va_d = dscr("Va_d", [S, 384])
    KaTo_d = dscr("KaTo_d", [6, 64, NO])
    Vao_d = dscr("Vao_d", [NO, 384])
    KdT_d = dscr("KdT_d", [6, 64, NW])
    Vd_d = dscr("Vd_d", [NW, 384])
    QaT_d = dscr("QaT_d", [6, 64, NQ])
    QdT_d = dscr("QdT_d", [6, 64, NQ])
    QmT_d = dscr("QmT_d", [4, 64, NQ])
    sig_d = dscr("sig_d", [24, 128, NQ])
    MkT_d = dscr("MkT_d", [4, 64, 256])
    Mv_d = dscr("Mv_d", [256, 256])
    x1_d = dscr("x1_d", [NQ, D], F32)

    def sb(name, shape, dt):
        return es.enter_context(nc.sbuf_tensor(name, list(shape), dt))

    def ring(name, shape, dt, n, stack=None):
        st = stack or es
        return Ring([st.enter_context(nc.sbuf_tensor(f"{name}{i}", list(shape), dt)) for i in range(n)])

    ps = [es.enter_context(nc.psum_tensor(f"ps{i}", [128, 512], F32)) for i in range(8)]
    psr = [Res() for _ in range(8)]

    ident = sb("ident", [128, 128], BF16)
    tri = sb("tri", [128, 128], BF16)
    invf = sb("invf", [128, 64], F32)
    offs = sb("offs", [128, 64], F32)
    g6s = sb("g6s", [128, 6, 64], F32)
    gmix = sb("gmix", [128, D], F32)
    cres = Res()
    fence = sb("fence", [128, 2], F32)
    P.op("dve", lambda e: e.memset(fence[:], 0.0), writes=[Res()])
    epsc = sb("epsc", [128, 1], F32)
    P.op("dve", lambda e: e.memset(epsc[:], EPS), writes=[cres])
    for dst, src in ((ident, c_ident), (tri, c_tri), (invf, c_invf), (offs, c_offs), (g6s, g6), (gmix, g_mix)):
        P.dma("sp", (lambda d, s_: lambda e: e.dma_start(out=d[:], in_=s_))(dst, src), writes=[cres])

    def load_w(wsb, wres, wsrc, r0, r1, c0, c1, col_off=0):
        nk = (r1 - r0) // 128
        for kf in range(nk):
            P.dma("pool", (lambda kf=kf: lambda e: e.dma_start(
                out=wsb[:, kf, col_off:col_off + (c1 - c0)],
                in_=wsrc[r0 + kf * 128:r0 + (kf + 1) * 128, c0:c1]))(), writes=[wres])

    def rms_rstd(stack_tiles, src_ap, src_res, n, width, split):
        sq, ss, rs = stack_tiles
        sqt, sqr = sq.next()
        sst, ssr = ss.next()
        rst, rsr = rs.next()
        P.op("act", lambda e: e.activation(out=sqt[:, 0:n * width], in_=src_ap, func=AF.Square),
             reads=[src_res], writes=[sqr])
        if split:
            yield
        P.op("dve", lambda e: e.tensor_reduce(out=sst[:, 0:n], in_=sqt[:, 0:n * width].rearrange("p (n w) -> p n w", w=width),
                                              axis=AX.X, op=ALU.add), reads=[sqr], writes=[ssr])
        if split:
            yield
        P.op("act", lambda e: e.activation(out=sst[:, 0:n], in_=sst[:, 0:n], func=AF.Sqrt, scale=1.0 / width, bias=epsc[:, 0:1]),
             reads=[ssr, cres], writes=[ssr])
        if split:
            yield
        P.op("dve", lambda e: e.reciprocal(out=rst[:, 0:n], in_=sst[:, 0:n]), reads=[ssr], writes=[rsr])
        return rst, rsr

    with ExitStack() as ph:
        RG = {}

        def set_rings(st, tag, maxd, npf, nob):
            k = maxd
            RG["xt"] = ring("xt" + tag, [128, D], F32, 3, st)
            RG["junk"] = ring("junk" + tag, [128, D], BF16, 1, st)
            RG["ssx"] = ring("ssx" + tag, [128, 1], F32, 6, st)
            RG["xn"] = ring("xn" + tag, [128, D], BF16, 3, st)
            RG["hT"] = ring("hT" + tag, [128, 8, 128], BF16, k + 2, st)
            RG["pf"] = ring("pf" + tag, [128, 512], F32, npf, st)
            RG["sq"] = ring("sq" + tag, [128, 512], F32, 4, st)
            RG["ss"] = ring("ss" + tag, [128, 8], F32, 12, st)
            RG["rs"] = ring("rs" + tag, [128, 8], F32, 12, st)
            RG["nrm"] = ring("nrm" + tag, [128, 6, 64], F32, 4, st)
            RG["rt"] = ring("rt" + tag, [128, 2, 6, 64], F32, 4, st)
            RG["ob"] = ring("ob" + tag, [128, 384], BF16, nob, st)
            RG["vob"] = ring("vob" + tag, [128, 384], BF16, 3, st)
            RG["st"] = ring("st" + tag, [128, 3, 128], BF16, 6, st)
            RG["gsb"] = ring("gsb" + tag, [128, 512], BF16, 4, st)

        def build_tabs(tabs, tabs_res, posT_src, nt, gfold, tag, barrier=True):
            with ExitStack() as tmp:
                posi = tmp.enter_context(nc.sbuf_tensor("posi" + tag, [128, nt], I32))
                posf = tmp.enter_context(nc.sbuf_tensor("posf" + tag, [128, nt], F32))
                CH = 8
                A = tmp.enter_context(nc.sbuf_tensor("bA" + tag, [128, CH, 64], F32))
                T = tmp.enter_context(nc.sbuf_tensor("bT" + tag, [128, CH, 64], F32))
                Kf = tmp.enter_context(nc.sbuf_tensor("bK" + tag, [128, CH, 64], F32))
                Ki = tmp.enter_context(nc.sbuf_tensor("bKi" + tag, [128, CH, 64], I32))
                r = Res()
                P.dma("sp", lambda e: e.dma_start(out=posi[:], in_=posT_src), writes=[r])
                P.op("dve", lambda e: e.tensor_copy(out=posf[:], in_=posi[:]), reads=[r], writes=[r])
                for c0 in range(0, nt, CH):
                    n = min(CH, nt - c0)
                    a_, t_, k_, ki_ = A[:, 0:n, :], T[:, 0:n, :], Kf[:, 0:n, :], Ki[:, 0:n, :]
                    P.op("dve", (lambda a_=a_, c0=c0, n=n: lambda e: e.tensor_tensor(
                        out=a_, in0=posf[:, c0:c0 + n].unsqueeze(2).to_broadcast([128, n, 64]),
                        in1=invf[:].unsqueeze(1).to_broadcast([128, n, 64]), op=ALU.mult))(), reads=[r, cres], writes=[r])
                    P.op("dve", (lambda a_=a_: lambda e: e.tensor_scalar(out=a_[:, :, 32:64], in0=a_[:, :, 32:64], scalar1=math.pi / 2,
                                                                         scalar2=None, op0=ALU.add))(), reads=[r], writes=[r])
                    P.op("dve", (lambda a_=a_, t_=t_: lambda e: e.tensor_scalar(out=t_, in0=a_, scalar1=1.0 / (2 * math.pi), scalar2=0.5,
                                                                                op0=ALU.mult, op1=ALU.add))(), reads=[r], writes=[r])
                    P.op("dve", (lambda ki_=ki_, t_=t_: lambda e: e.tensor_copy(out=ki_, in_=t_))(), reads=[r], writes=[r])
                    P.op("dve", (lambda ki_=ki_, k_=k_: lambda e: e.tensor_copy(out=k_, in_=ki_))(), reads=[r], writes=[r])
                    P.op("dve", (lambda ki_=ki_, k_=k_, t_=t_: lambda e: e.tensor_tensor(out=t_, in0=t_, in1=k_, op=ALU.is_lt))(),
                         reads=[r], writes=[r])
                    P.op("dve", (lambda k_=k_, t_=t_: lambda e: e.tensor_tensor(out=k_, in0=k_, in1=t_, op=ALU.subtract))(),
                         reads=[r], writes=[r])
                    P.op("dve", (lambda k_=k_, a_=a_: lambda e: e.scalar_tensor_tensor(out=a_, in0=k_, scalar=-2.0 * math.pi, in1=a_,
                                                                                       op0=ALU.mult, op1=ALU.add))(), reads=[r], writes=[r])
                    P.op("dve", (lambda a_=a_: lambda e: e.tensor_scalar(out=a_, in0=a_, scalar1=-3.141592, scalar2=3.141592,
                                                                         op0=ALU.max, op1=ALU.min))(), reads=[r], writes=[r])
                    P.op("act", (lambda a_=a_, t_=t_: lambda e: e.activation(out=t_, in_=a_, func=AF.Sin))(), reads=[r], writes=[r])
                    tb = tabs[:, c0:c0 + n, :, :]
                    if gfold is None:
                        P.op("dve", (lambda tb=tb, t_=t_: lambda e: e.tensor_copy(out=tb[:, :, 0, 0:32], in_=t_[:, :, 32:64]))(), reads=[r], writes=[tabs_res])
                        P.op("dve", (lambda tb=tb, t_=t_: lambda e: e.tensor_copy(out=tb[:, :, 0, 32:64], in_=t_[:, :, 32:64]))(), reads=[r], writes=[tabs_res])
                        P.op("dve", (lambda tb=tb, t_=t_: lambda e: e.tensor_copy(out=tb[:, :, 1, 32:64], in_=t_[:, :, 0:32]))(), reads=[r], writes=[tabs_res])
                        P.op("dve", (lambda tb=tb, t_=t_: lambda e: e.tensor_scalar(out=tb[:, :, 1, 0:32], in0=t_[:, :, 0:32], scalar1=-1.0, scalar2=None,
                                                                                   op0=ALU.mult))(), reads=[r], writes=[tabs_res])
                    else:
                        g1 = g6s[:, gfold, 0:32].unsqueeze(1).to_broadcast([128, n, 32])
                        g2 = g6s[:, gfold, 32:64].unsqueeze(1).to_broadcast([128, n, 32])
                        P.op("dve", (lambda tb=tb, t_=t_, g1=g1: lambda e: e.tensor_tensor(out=tb[:, :, 0, 0:32], in0=t_[:, :, 32:64], in1=g1, op=ALU.mult))(),
                             reads=[r, cres], writes=[tabs_res])
                        P.op("dve", (lambda tb=tb, t_=t_, g2=g2: lambda e: e.tensor_tensor(out=tb[:, :, 0, 32:64], in0=t_[:, :, 32:64], in1=g2, op=ALU.mult))(),
                             reads=[r, cres], writes=[tabs_res])
                        P.op("dve", (lambda tb=tb, t_=t_, g1=g1: lambda e: e.tensor_tensor(out=tb[:, :, 1, 32:64], in0=t_[:, :, 0:32], in1=g1, op=ALU.mult))(),
                             reads=[r, cres], writes=[tabs_res])
                        P.op("dve", (lambda tb=tb, t_=t_, g2=g2: lambda e: e.scalar_tensor_tensor(out=tb[:, :, 1, 0:32], in0=t_[:, :, 0:32], scalar=-1.0, in1=g2,
                                                                                                 op0=ALU.mult, op1=ALU.mult))(),
                             reads=[r, cres], writes=[tabs_res])
                if barrier:
                    plan_barrier(P)

        def frontend(xsrc, t, gtile=None):
            gtile = gmix if gtile is None else gtile
            xt, xtr = RG["xt"].next()
            P.dma("sp", lambda e: e.dma_start(out=xt[:], in_=xsrc[t * 128:(t + 1) * 128, :]), writes=[xtr])
            jk, jkr = RG["junk"].next()
            ssx, ssxr = RG["ssx"].next()
            P.op("act", lambda e: e.activation(out=jk[:], in_=xt[:], func=AF.Square, accum_out=ssx[:]),
                 reads=[xtr], writes=[jkr, ssxr])
            P.op("act", lambda e: e.copy(out=fence[:, 0:1], in_=fence[:, 1:2]), writes=[ssxr])
            P.op("act", lambda e: e.activation(out=ssx[:], in_=ssx[:], func=AF.Sqrt, scale=1.0 / D, bias=epsc[:, 0:1]),
                 reads=[ssxr, cres], writes=[ssxr])
            P.op("dve", lambda e: e.reciprocal(out=ssx[:], in_=ssx[:]), reads=[ssxr], writes=[ssxr])
            yield
            xn, xnr = RG["xn"].next()
            P.op("dve", lambda e: e.scalar_tensor_tensor(out=xn[:], in0=xt[:], scalar=ssx[:, 0:1], in1=gtile[:],
                                                         op0=ALU.mult, op1=ALU.mult),
                 reads=[xtr, ssxr, cres], writes=[xnr])
            yield
            for kf in range(8):
                b = kf // 4
                P.op("pe", (lambda kf=kf, b=b: lambda e: e.matmul(ps[b][:, (kf % 4) * 128:(kf % 4 + 1) * 128],
                                                                   lhsT=xn[:, kf * 128:(kf + 1) * 128], rhs=ident[:],
                                                                   start=True, stop=True))(),
                     reads=[xnr, cres], writes=[psr[b]])
            hT, hTr = RG["hT"].next()
            P.op("act", lambda e: e.copy(out=hT[:, 0:4, :].rearrange("p a b -> p (a b)"), in_=ps[0][:]),
                 reads=[psr[0]], writes=[hTr])
            P.op("dve", lambda e: e.tensor_copy(out=hT[:, 4:8, :].rearrange("p a b -> p (a b)"), in_=ps[1][:]),
                 reads=[psr[1]], writes=[hTr])
            return hT, hTr

        pbc = [0]

        def nb():
            b = 2 + pbc[0] % 6
            pbc[0] += 1
            return b

        def proj(W, Wres, hT, hTr, c0, n, bank):
            for kf in range(8):
                P.op("pe", (lambda kf=kf: lambda e: e.matmul(ps[bank][:, 0:n], lhsT=hT[:, kf, :], rhs=W[:, kf, c0:c0 + n],
                                                             start=(kf == 0), stop=(kf == 7)))(),
                     reads=[hTr, Wres], writes=[psr[bank]])

        def proj_evac(W, Wres, hT, hTr, c0, n):
            b = nb()
            proj(W, Wres, hT, hTr, c0, n, b)
            pf, pfr = RG["pf"].next()
            P.op("act", lambda e: e.copy(out=pf[:, 0:n], in_=ps[b][:, 0:n]), reads=[psr[b]], writes=[pfr])
            return pf, pfr

        def v_to_dram(bank, dst, row0, c0=0, n=384):
            ob, obr = RG["vob"].next()
            P.op("act", lambda e: e.copy(out=ob[:, 0:n], in_=ps[bank][:, c0:c0 + n]), reads=[psr[bank]], writes=[obr])
            P.dma("act", lambda e: e.dma_start(out=dst[row0:row0 + 128, :], in_=ob[:, 0:n]), reads=[obr])

        def proj_v(W, Wres, hT, hTr, c0, n, dst, row0):
            b = nb()
            proj(W, Wres, hT, hTr, c0, n, b)
            v_to_dram(b, dst, row0, 0, n)

        def head_post(pfin, nh, tabs, tabs_res, t, scale, gain_idx, rope=True, split=False):
            pf, pfr = pfin
            n = nh * 64
            rst, rsr = yield from rms_rstd((RG["sq"], RG["ss"], RG["rs"]), pf[:, 0:n], pfr, nh, 64, split)
            nm, nmr = RG["nrm"].next()
            pf3 = pf[:, 0:n].rearrange("p (h d) -> p h d", d=64)
            P.op("dve", lambda e: e.tensor_tensor(out=nm[:, 0:nh, :], in0=pf3,
                                                  in1=rst[:, 0:nh].unsqueeze(2).to_broadcast([128, nh, 64]), op=ALU.mult),
                 reads=[pfr, rsr], writes=[nmr])
            ob, obr = RG["ob"].next()
            ob3 = ob[:, 0:n].rearrange("p (h d) -> p h d", d=64)
            if not rope:
                P.op("dve", lambda e: e.scalar_tensor_tensor(
                    out=ob3, in0=nm[:, 0:nh, :], scalar=scale,
                    in1=g6s[:, gain_idx, :].unsqueeze(1).to_broadcast([128, nh, 64]), op0=ALU.mult, op1=ALU.mult),
                     reads=[nmr, cres], writes=[obr])
                return ob, obr
            if gain_idx is not None:
                P.op("dve", lambda e: e.scalar_tensor_tensor(
                    out=nm[:, 0:nh, :], in0=nm[:, 0:nh, :], scalar=scale,
                    in1=g6s[:, gain_idx, :].unsqueeze(1).to_broadcast([128, nh, 64]), op0=ALU.mult, op1=ALU.mult)
                    if False else e.tensor_tensor(out=nm[:, 0:nh, :], in0=nm[:, 0:nh, :],
                                                  in1=gsc[(gain_idx, scale)][:].unsqueeze(1).to_broadcast([128, nh, 64]), op=ALU.mult),
                     reads=[nmr, cres], writes=[nmr])
            rt, rtr = RG["rt"].next()
            cc = tabs[:, t, 0, :].unsqueeze(1).to_broadcast([128, nh, 64])
            ss_ = tabs[:, t, 1, :].unsqueeze(1).to_broadcast([128, nh, 64])
            nsw = nm[:, 0:nh, :].rearrange("p h (two d) -> p h two d", two=2)[:, :, ::-1, :]
            P.op("dve", lambda e: e.tensor_tensor(out=rt[:, 0, 0:nh, :], in0=nm[:, 0:nh, :], in1=cc, op=ALU.mult),
                 reads=[nmr, tabs_res], writes=[rtr])
            P.op("dve", lambda e: e.tensor_tensor(out=rt[:, 1, 0:nh, :].rearrange("p h (two d) -> p h two d", two=2), in0=nsw,
                                                   in1=ss_.rearrange("p h (two d) -> p h two d", two=2) if False else
                                                   tabs[:, t, 1, :].rearrange("p (two d) -> p two d", two=2).unsqueeze(1).to_broadcast([128, nh, 2, 32]),
                                                   op=ALU.mult),
                 reads=[nmr, tabs_res], writes=[rtr])
            P.op("dve", lambda e: e.tensor_tensor(out=ob3, in0=rt[:, 0, 0:nh, :], in1=rt[:, 1, 0:nh, :], op=ALU.add),
                 reads=[rtr], writes=[obr])
            return ob, obr

        def to_T_dram(ob, obr, nh, dst, col0):
            bank = nb()
            npair = nh // 2
            for hp in range(npair):
                P.op("pe", (lambda hp=hp: lambda e: e.matmul(ps[bank][:, hp * 128:(hp + 1) * 128],
                                                             lhsT=ob[:, hp * 128:(hp + 1) * 128], rhs=ident[:],
                                                             start=True, stop=True))(),
                     reads=[obr, cres], writes=[psr[bank]])
            st, str_ = RG["st"].next()
            P.op("act", lambda e: e.copy(out=st[:, 0:npair, :].rearrange("p a b -> p (a b)"), in_=ps[bank][:, 0:npair * 128]),
                 reads=[psr[bank]], writes=[str_])
            P.dma("act", lambda e: e.dma_start(
                out=dst.rearrange("(a two) d n -> (two d) a n", two=2)[:, :, col0:col0 + 128], in_=st[:, 0:npair, :]),
                reads=[str_])

        gsc = {}
        for key in ((3, 1.0), (1, 1.0), (0, 0.125), (2, 0.125)):
            gt = ph.enter_context(nc.sbuf_tensor(f"gsc{key[0]}", [128, 64], F32))
            gsc[key] = gt
            P.op("dve", (lambda gt=gt, key=key: lambda e: e.tensor_scalar(out=gt[:], in0=g6s[:, key[0], :], scalar1=key[1], scalar2=None,
                                                                         op0=ALU.mult))(), reads=[cres], writes=[cres])

        with ExitStack() as pa:
            MAXD = 10
            set_rings(pa, "A", MAXD, MAXD + 2, MAXD + 2)
            WA = pa.enter_context(nc.sbuf_tensor("WA", [128, 8, 768], BF16))
            WAres = Res()
            load_w(WA, WAres, w_in, 0, 1024, 384, 1152)
            tabsA = pa.enter_context(nc.sbuf_tensor("tabsA", [128, S // 128, 2, 64], F32))
            tabsA_res = Res()
            build_tabs(tabsA, tabsA_res, pos_allT, S // 128, 1, "A", barrier=False)

            def tileA(t):
                hT, hTr = yield from frontend(xall, t)
                yield
                pfk = proj_evac(WA, WAres, hT, hTr, 0, 384)
                proj_v(WA, WAres, hT, hTr, 384, 384, Va_d, t * 128)
                yield
                ob, obr = yield from head_post(pfk, 6, tabsA, tabsA_res, t, 1.0, None, split=True)
                yield
                to_T_dram(ob, obr, 6, KaT_d, t * 128)
            run_pipeline((tileA(t) for t in range(S // 128)), MAXD)
        plan_barrier(P)

        MAXD = 8
        set_rings(ph, "B", MAXD, MAXD + 2, MAXD + 2)
        tabsB = ph.enter_context(nc.sbuf_tensor("tabsB", [128, NWT, 2, 64], F32))
        tabsB_res = Res()
        build_tabs(tabsB, tabsB_res, pos_winT, NWT, None, "B")
        W = ph.enter_context(nc.sbuf_tensor("W", [128, 8, 2560], BF16))
        Wres = Res()
        hTall = ph.enter_context(nc.sbuf_tensor("hTall", [128, 8, NQ], BF16))
        hTall_res = Res()
        load_w(W, Wres, w_in, 0, 1024, 0, 2560)

        def tileB(t):
            own = t >= 15
            isq = t >= 16
            qt = t - 16
            hT, hTr = yield from frontend(xwin, t)
            yield
            proj_v(W, Wres, hT, hTr, 1920, 384, Vd_d, t * 128)
            if isq:
                P.dma("sp", (lambda qt=qt, hT=hT: lambda e: e.dma_start(out=hTall[:, :, qt * 128:(qt + 1) * 128], in_=hT[:]))(),
                      reads=[hTr], writes=[hTall_res])
            groups = [(1536, 6, 3, 1.0, KdT_d, t * 128, True)]
            if own:
                groups.append((384, 6, 1, 1.0, KaTo_d, (t - 15) * 128, True))
            if isq:
                groups += [(0, 6, 0, 0.125, QaT_d, qt * 128, True), (1152, 6, 2, 0.125, QdT_d, qt * 128, True),
                           (2304, 4, 4, 0.125, QmT_d, qt * 128, False)]
            first = True
            for (c0, nh, gidx, scl, dst, dcol, rope) in groups:
                pfk = proj_evac(W, Wres, hT, hTr, c0, nh * 64)
                if first and own:
                    proj_v(W, Wres, hT, hTr, 768, 384, Vao_d, (t - 15) * 128)
                first = False
                yield
                ob, obr = yield from head_post(pfk, nh, tabsB if rope else None, tabsB_res if rope else None, t, scl, gidx,
                                               rope=rope, split=True)
                yield
                to_T_dram(ob, obr, nh, dst, dcol)
        run_pipeline((tileB(t) for t in range(NWT)), MAXD)

        plan_barrier(P)
        groups = [(0, 512), (512, 512), (1024, 512), (1536, 512), (2048, 128)]
        gw = [Res(), Res()]

        def load_gw(part):
            hb = part % 2
            for kf in range(8):
                P.dma("pool", (lambda kf=kf, hb=hb, part=part: lambda e: e.dma_start(
                    out=W[:, kf, hb * 1024:(hb + 1) * 1024],
                    in_=w_in[kf * 128:(kf + 1) * 128, 2560 + part * 1024:2560 + (part + 1) * 1024]))(), writes=[gw[hb]])
        load_gw(0)
        load_gw(1)
        gi_ = 0
        for cc in range(24):
            part, cl = cc // 8, cc % 8
            hb = part % 2
            for (g0, gn) in groups:
                bank = 2 + (gi_ % 6)
                gi_ += 1
                for kf in range(8):
                    P.op("pe", (lambda kf=kf, hb=hb, cl=cl, g0=g0, gn=gn, bank=bank: lambda e: e.matmul(
                        ps[bank][:, 0:gn], lhsT=W[:, kf, hb * 1024 + cl * 128:hb * 1024 + (cl + 1) * 128], rhs=hTall[:, kf, g0:g0 + gn],
                        start=(kf == 0), stop=(kf == 7)))(), reads=[gw[hb], hTall_res], writes=[psr[bank]])
                gs_, gsr = RG["gsb"].next()
                P.op("act", (lambda gs_=gs_, gn=gn, bank=bank: lambda e: e.activation(out=gs_[:, 0:gn], in_=ps[bank][:, 0:gn], func=AF.Sigmoid))(),
                     reads=[psr[bank]], writes=[gsr])
                P.dma("act", (lambda gs_=gs_, cc=cc, g0=g0, gn=gn: lambda e: e.dma_start(out=sig_d[cc, :, g0:g0 + gn], in_=gs_[:, 0:gn]))(),
                      reads=[gsr])
            if cc == 7:
                load_gw(2)
        plan_barrier(P)

        gmem = ph.enter_context(nc.sbuf_tensor("gmem", [128, D], F32))
        P.dma("sp", lambda e: e.dma_start(out=gmem[:], in_=g_mem), writes=[cres])
        load_w(W, Wres, w_mkv, 0, 1024, 0, 512)

        def tileC(t):
            hT, hTr = yield from frontend(mem, t, gtile=gmem)
            b1 = nb()
            proj(W, Wres, hT, hTr, 0, 512, b1)
            pf, pfr = RG["pf"].next()
            P.op("act", lambda e: e.copy(out=pf[:, 0:256], in_=ps[b1][:, 0:256]), reads=[psr[b1]], writes=[pfr])
            v_to_dram(b1, Mv_d, t * 128, 256, 256)
            yield
            ob, obr = yield from head_post((pf, pfr), 4, None, None, t, 1.0, 5, rope=False)
            yield
            to_T_dram(ob, obr, 4, MkT_d, t * 128)
        run_pipeline((tileC(t) for t in range(2)), MAXD)

    dres = Res()
    plan_barrier(P)

    O_BANK0 = 3

    def oacc(qt):
        b = O_BANK0 + qt // 7
        c0 = (qt % 7) * 65
        return ps[b][:, c0:c0 + 65], psr[b]

    attnT = es.enter_context(nc.sbuf_tensor("attnT", [128, 8, NQ], BF16))
    attnT_res = Res()
    with ExitStack() as at:
        attn = at.enter_context(nc.sbuf_tensor("attn", [128, NQT, 1024], BF16))
        attn_res = Res()
        pt_r = ring("pt", [128, 512], BF16, 4, at)
        den_r = ring("den", [128, 16], F32, 4, at)
        tri2 = at.enter_context(nc.sbuf_tensor("tri2", [128, 256], BF16))
        P.op("dve", lambda e: e.memset(tri2[:, 0:128], 1.0), writes=[cres])
        P.dma("sp", lambda e: e.dma_start(out=tri2[:, 128:256], in_=c_tri), writes=[cres])

        sst = {"i": 0, "pend": []}
        LAG = 2

        def flush(keep):
            while len(sst["pend"]) > keep:
                for args in sst["pend"].pop(0):
                    P.op(*args[0], **args[1])

        def step(qk, n, pv, mask=None, mask_res=None):
            b = sst["i"] % 3
            sst["i"] += 1
            for (c0, w, lhsT, rhs, rres) in qk:
                P.op("pe", (lambda c0=c0, w=w, lhsT=lhsT, rhs=rhs, b=b: lambda e: e.matmul(
                    ps[b][:, c0:c0 + w], lhsT=lhsT, rhs=rhs, start=True, stop=True))(), reads=rres, writes=[psr[b]])
            pt, ptr = pt_r.next()
            P.op("act", (lambda pt=pt, b=b, n=n: lambda e: e.activation(out=pt[:, 0:n], in_=ps[b][:, 0:n], func=AF.Exp))(),
                 reads=[psr[b]], writes=[ptr])
            if mask is not None:
                P.op("dve", (lambda pt=pt, n=n, mask=mask: lambda e: e.tensor_tensor(
                    out=pt[:, 0:n], in0=pt[:, 0:n], in1=mask, op=ALU.mult))(), reads=[ptr, mask_res], writes=[ptr])
            pvl = []
            for (c0, vap, vres, qt) in pv:
                o, ores = oacc(qt)
                pvl.append((("pe", (lambda o=o, pt=pt, c0=c0, vap=vap: lambda e: e.matmul(
                    o, lhsT=pt[:, c0:c0 + 128], rhs=vap, start=False, stop=False, skip_group_check=True))()),
                    dict(reads=[ptr, vres], writes=[ores])))
            sst["pend"].append(pvl)
            flush(LAG)

        def zero_acc():
            flush(0)
            for b in range(3):
                P.op("dve", (lambda b=b: lambda e: e.memset(ps[O_BANK0 + b][:], 0.0))(), writes=[psr[O_BANK0 + b]])

        def finalize(col0):
            flush(0)
            for b in range(3):
                q0 = b * 7
                n = min(7, NQT - q0)
                bank = ps[O_BANK0 + b]
                ores = psr[O_BANK0 + b]
                o3 = bank[:, 0:n * 65].rearrange("p (t c) -> p t c", c=65)
                dn, dnr = den_r.next()
                P.op("dve", (lambda o3=o3, dn=dn, n=n: lambda e: e.tensor_scalar(
                    out=dn[:, 0:n], in0=o3[:, :, 64], scalar1=1e-30, scalar2=None, op0=ALU.max))(), reads=[ores], writes=[dnr])
                P.op("dve", (lambda dn=dn, n=n: lambda e: e.reciprocal(out=dn[:, 8:8 + n], in_=dn[:, 0:n]))(), reads=[dnr], writes=[dnr])
                P.op("dve", (lambda o3=o3, dn=dn, n=n, q0=q0: lambda e: e.tensor_tensor(
                    out=attn[:, q0:q0 + n, col0:col0 + 64], in0=o3[:, :, 0:64],
                    in1=dn[:, 8:8 + n].unsqueeze(2).to_broadcast([128, n, 64]), op=ALU.mult))(),
                     reads=[ores, dnr], writes=[attn_res])

        with ExitStack() as mo:
            kt = [mo.enter_context(nc.sbuf_tensor(f"kt{i}", [128, S], BF16)) for i in range(2)]
            ktr = [Res(), Res()]
            va = [mo.enter_context(nc.sbuf_tensor(f"vaug{i}", [128, 128, 65], BF16)) for i in range(2)]
            var = [Res(), Res()]
            qa_ = [mo.enter_context(nc.sbuf_tensor(f"qaug{i}", [128, NQ], BF16)) for i in range(2)]
            qar = [Res(), Res()]
            kto = [mo.enter_context(nc.sbuf_tensor(f"kto{i}", [128, NO], BF16)) for i in range(2)]
            ktor = [Res(), Res()]
            vo = [mo.enter_context(nc.sbuf_tensor(f"vo{i}", [128, NOT, 65], BF16)) for i in range(2)]
            vor = [Res(), Res()]
            gmask = mo.enter_context(nc.sbuf_tensor("gmask", [128, NQT, 64], F32))
            P.dma("sp", lambda e: e.dma_start(out=gmask[:], in_=c_gmask), writes=[cres])
            km32 = mo.enter_context(nc.sbuf_tensor("km32", [128, 64], F32))
            kmT = mo.enter_context(nc.sbuf_tensor("kmT", [128, 64], BF16))
            kmr = Res()
            g32_r = ring("g32", [128, 64], F32, 2, mo)
            t8_r = ring("t8", [128, 16], F32, 2, mo)
            bq_r = ring("bq", [128, 128], BF16, 2, mo)
            for i in range(2):
                P.dma("act" if i else "sp", (lambda i=i: lambda e: e.dma_start(out=kt[i][64:128, :], in_=c_tmat))(), writes=[ktr[i]])
                P.op("pool", (lambda i=i: lambda e: e.memset(va[i][:, :, 64:65], 1.0))(), writes=[var[i]])
                P.op("pool", (lambda i=i: lambda e: e.memset(vo[i][:, :, 64:65], 1.0))(), writes=[vor[i]])
                P.op("pool", (lambda i=i: lambda e: e.memset(kto[i][64:128, :], 0.0))(), writes=[ktor[i]])
                bq, bqr = bq_r.next()
                P.op("pool", (lambda bq=bq: lambda e: e.memset(bq[:, 0:64], 0.0))(), writes=[bqr])

            def moba_load(h):
                i = h % 2
                for qq in range(4):
                    P.dma("sp" if qq % 2 == 0 else "act", (lambda qq=qq, i=i, h=h: lambda e: e.dma_start(
                        out=kt[i][0:64, qq * 4096:(qq + 1) * 4096], in_=KaT_d[h, :, qq * 4096:(qq + 1) * 4096]))(),
                        writes=[ktr[i]])
                vsrc = Va_d.rearrange("(ch p) c -> p ch c", p=128)
                for qq in range(16):
                    P.dma("sp" if qq % 2 == 0 else "act", (lambda qq=qq, i=i, h=h: lambda e: e.dma_start(
                        out=va[i][:, qq * 8:(qq + 1) * 8, 0:64], in_=vsrc[:, qq * 8:(qq + 1) * 8, h * 64:(h + 1) * 64]))(),
                        writes=[var[i]])
                P.dma("sp", (lambda i=i, h=h: lambda e: e.dma_start(out=qa_[i][0:64, :], in_=QaT_d[h]))(), writes=[qar[i]])
                P.dma("act", (lambda i=i, h=h: lambda e: e.dma_start(out=kto[i][0:64, :], in_=KaTo_d[h]))(), writes=[ktor[i]])
                for qq in range(3):
                    P.dma("sp", (lambda i=i, h=h, qq=qq: lambda e: e.dma_start(
                        out=vo[i][:, qq * 6:(qq + 1) * 6, 0:64],
                        in_=Vao_d.rearrange("(ch p) c -> p ch c", p=128)[:, qq * 6:(qq + 1) * 6, h * 64:(h + 1) * 64]))(),
                        writes=[vor[i]])

            def gate_gen(h):
                i = h % 2
                KT, Q = kt[i], qa_[i]
                P.op("dve", (lambda KT=KT: lambda e: e.tensor_reduce(
                    out=km32[0:64, :], in_=KT[0:64, :].rearrange("p (n k) -> p n k", k=256), axis=AX.X, op=ALU.add))(),
                     reads=[ktr[i]], writes=[kmr])
                P.op("act", lambda e: e.copy(out=kmT[0:64, :], in_=km32[0:64, :]), reads=[kmr], writes=[kmr])
                yield
                for qt in range(NQT):
                    qs = slice(qt * 128, (qt + 1) * 128)
                    P.op("pe", (lambda Q=Q, qs=qs: lambda e: e.matmul(ps[6][:, 0:64], lhsT=Q[0:64, qs], rhs=kmT[0:64, :],
                                                                      start=True, stop=True))(),
                         reads=[qar[i], kmr], writes=[psr[6]])
                    g32, g32r = g32_r.next()
                    t8, t8r = t8_r.next()
                    bq, bqr = bq_r.next()
                    P.op("dve", (lambda g32=g32, qt=qt: lambda e: e.tensor_tensor(out=g32[:], in0=ps[6][:, 0:64], in1=gmask[:, qt, :],
                                                                                  op=ALU.add))(), reads=[psr[6], cres], writes=[g32r])
                    P.op("dve", (lambda g32=g32, t8=t8: lambda e: e.max(out=t8[:, 0:8], in_=g32[:]))(), reads=[g32r], writes=[t8r])
                    P.op("dve", (lambda t8=t8: lambda e: e.tensor_scalar(out=t8[:, 8:9], in0=t8[:, 2:3], scalar1=-1e8, scalar2=None,
                                                                         op0=ALU.max))(), reads=[t8r], writes=[t8r])
                    P.op("dve", (lambda g32=g32, t8=t8, bq=bq: lambda e: e.tensor_scalar(
                        out=bq[:, 64:128], in0=g32[:], scalar1=t8[:, 8:9], scalar2=1.0, op0=ALU.is_ge, op1=ALU.subtract))(),
                         reads=[g32r, t8r], writes=[bqr])
                    P.op("pe", (lambda bq=bq: lambda e: e.matmul(ps[7][:, 0:128], lhsT=bq[:], rhs=ident[:], start=True, stop=True))(),
                         reads=[bqr, cres], writes=[psr[7]])
                    P.op("act", (lambda Q=Q, qs=qs: lambda e: e.activation(out=Q[64:128, qs], in_=ps[7][64:128, 0:128],
                                                                           func=AF.Identity, scale=30000.0))(),
                         reads=[psr[7]], writes=[qar[i]])
                    yield

            moba_load(0)
            for _ in gate_gen(0):
                pass
            for h in range(6):
                i = h % 2
                gnext = None
                if h + 1 < 6:
                    flush(0)
                    moba_load(h + 1)
                    gnext = gate_gen(h + 1)
                KT, V, Q, KO, VO = kt[i], va[i], qa_[i], kto[i], vo[i]
                nstep = 0
                zero_acc()
                pieces = [(0, 512), (512, 512), (1024, 512), (1536, 512), (2048, 128)]
                for ch in range(128):
                    j = ch // 2
                    for pi_, (p0, n) in enumerate(pieces):
                        if j >= 57 + 2 * pi_ and pi_ < 4:
                            continue
                        if pi_ == 4 and j >= 63:
                            continue
                        qk = [(0, n, KT[:, ch * 128:(ch + 1) * 128], Q[:, p0:p0 + n], [ktr[i], qar[i]])]
                        pv = [(k * 128, V[:, ch, :], var[i], p0 // 128 + k) for k in range(n // 128)]
                        step(qk, n, pv)
                        nstep += 1
                        if gnext is not None and nstep % 30 == 0:
                            next(gnext, None)
                for qt in range(NQT):
                    o = 1 + qt
                    qs = slice(qt * 128, (qt + 1) * 128)
                    if qt % 2 == 1:
                        qk = [(0, 128, KO[:, o * 128:(o + 1) * 128], Q[:, qs], [ktor[i], qar[i]])]
                        pv = [(0, VO[:, o, :], vor[i], qt)]
                        step(qk, 128, pv, mask=tri2[:, 128:256], mask_res=cres)
                    else:
                        qk = [(0, 128, KO[:, (o - 1) * 128:o * 128], Q[:, qs], [ktor[i], qar[i]]),
                              (128, 128, KO[:, o * 128:(o + 1) * 128], Q[:, qs], [ktor[i], qar[i]])]
                        pv = [(0, VO[:, o - 1, :], vor[i], qt), (128, VO[:, o, :], vor[i], qt)]
                        step(qk, 256, pv, mask=tri2[:, 0:256], mask_res=cres)
                if gnext is not None:
                    for _ in gnext:
                        pass
                finalize(h * 64)

        flush(0)
        plan_barrier(P)
        with ExitStack() as di:
            kd_ = [di.enter_context(nc.sbuf_tensor(f"kds{i}", [128, NW], BF16)) for i in range(2)]
            kdr = [Res(), Res()]
            qd_ = [di.enter_context(nc.sbuf_tensor(f"qds{i}", [128, NQ], BF16)) for i in range(2)]
            qdr = [Res(), Res()]
            vd_ = [di.enter_context(nc.sbuf_tensor(f"vds{i}", [128, NWT, 65], BF16)) for i in range(2)]
            vdr = [Res(), Res()]
            mdil = di.enter_context(nc.sbuf_tensor("mdil", [128, 17, 128], BF16))
            kvalid = di.enter_context(nc.sbuf_tensor("kvalid", [128, NWT], F32))
            P.dma("sp", lambda e: e.dma_start(out=mdil[:], in_=c_mdil), writes=[cres])
            P.dma("sp", lambda e: e.dma_start(out=kvalid[:], in_=c_kvalid), writes=[cres])

            def dil_load(h):
                i = h % 2
                P.dma("sp", (lambda i=i, h=h: lambda e: e.dma_start(out=kd_[i][0:64, :], in_=KdT_d[h]))(), writes=[kdr[i]])
                P.dma("act", (lambda i=i, h=h: lambda e: e.dma_start(out=qd_[i][0:64, :], in_=QdT_d[h]))(), writes=[qdr[i]])
                P.op("pool", (lambda i=i: lambda e: e.memset(vd_[i][:, :, 64:65], 1.0))(), writes=[vdr[i]])
                for c0_, c1_ in ((0, 8), (8, 16), (16, 24), (24, 33)):
                    P.dma("sp", (lambda i=i, h=h, c0_=c0_, c1_=c1_: lambda e: e.dma_start(
                        out=vd_[i][:, c0_:c1_, 0:64], in_=Vd_d.rearrange("(ch p) c -> p ch c", p=128)[:, c0_:c1_, h * 64:(h + 1) * 64]))(),
                        writes=[vdr[i]])
                P.op("dve", (lambda i=i: lambda e: e.tensor_tensor(
                    out=vd_[i][:], in0=vd_[i][:], in1=kvalid[:].unsqueeze(2).to_broadcast([128, NWT, 65]), op=ALU.mult))(),
                     reads=[vdr[i], cres], writes=[vdr[i]])

            dil_load(0)
            for h in range(6):
                i = h % 2
                if h + 1 < 6:
                    flush(0)
                    dil_load(h + 1)
                zero_acc()
                for qt in range(NQT):
                    w = 16 + qt
                    qs = slice(qt * 128, (qt + 1) * 128)
                    for d0 in (0, 4, 8, 12, 16):
                        cnt = 4 if d0 < 16 else 1
                        qk = [(k * 128, 128, kd_[i][0:64, (w - d0 - k) * 128:(w - d0 - k + 1) * 128], qd_[i][0:64, qs],
                               [kdr[i], qdr[i]]) for k in range(cnt)]
                        pv = [(k * 128, vd_[i][:, w - d0 - k, :], vdr[i], qt) for k in range(cnt)]
                        step(qk, cnt * 128, pv, mask=mdil[:, d0:d0 + cnt, :].rearrange("p a b -> p (a b)"), mask_res=cres)
                finalize(384 + h * 64)

        flush(0)
        plan_barrier(P)
        with ExitStack() as me:
            mk_ = [me.enter_context(nc.sbuf_tensor(f"mks{i}", [128, 256], BF16)) for i in range(2)]
            mkr = [Res(), Res()]
            qm_ = [me.enter_context(nc.sbuf_tensor(f"qms{i}", [128, NQ], BF16)) for i in range(2)]
            qmr = [Res(), Res()]
            mv_ = me.enter_context(nc.sbuf_tensor("mvs", [128, 2, 4, 65], BF16))
            mvr = Res()
            P.op("pool", lambda e: e.memset(mv_[:].rearrange("p a h c -> p (a h) c")[:, :, 64:65], 1.0), writes=[mvr])
            for chh in range(2):
                P.dma("sp", (lambda chh=chh: lambda e: e.dma_start(
                    out=mv_[:, chh, :, 0:64], in_=Mv_d[chh * 128:(chh + 1) * 128, :].rearrange("p (h d) -> p h d", d=64)))(),
                    writes=[mvr])
            for h in range(4):
                i = h % 2
                flush(0)
                P.dma("sp", (lambda i=i, h=h: lambda e: e.dma_start(out=mk_[i][0:64, :], in_=MkT_d[h]))(), writes=[mkr[i]])
                P.dma("act", (lambda i=i, h=h: lambda e: e.dma_start(out=qm_[i][0:64, :], in_=QmT_d[h]))(), writes=[qmr[i]])
                zero_acc()
                for qt in range(NQT):
                    qs = slice(qt * 128, (qt + 1) * 128)
                    qk = [(k * 128, 128, mk_[i][0:64, k * 128:(k + 1) * 128], qm_[i][0:64, qs], [mkr[i], qmr[i]]) for k in range(2)]
                    pv = [(k * 128, mv_[:, k, h, :], mvr, qt) for k in range(2)]
                    step(qk, 256, pv)
                finalize(768 + h * 64)
        flush(0)
        plan_barrier(P)

        if "attn_dbg" in DEBUG_OUT:
            attn_dbg = dscr("attn_dbg", [NQ, 1024])
            P.dma("sp", lambda e: e.dma_start(out=attn_dbg.rearrange("(t p) c -> p t c", p=128), in_=attn[:]), reads=[attn_res])
        for qt in range(NQT):
            for a in range(2):
                b = (2 * qt + a) % 3
                for k in range(4):
                    fc = a * 4 + k
                    P.op("pe", (lambda qt=qt, fc=fc, b=b, k=k: lambda e: e.matmul(
                        ps[b][:, k * 128:(k + 1) * 128], lhsT=attn[:, qt, fc * 128:(fc + 1) * 128], rhs=ident[:],
                        start=True, stop=True))(), reads=[attn_res, cres], writes=[psr[b]])
                P.op("act" if a == 0 else "dve", (lambda qt=qt, a=a, b=b: (
                    (lambda e: e.copy(out=attnT[:, a * 4:(a + 1) * 4, qt * 128:(qt + 1) * 128],
                                      in_=ps[b][:].rearrange("p (a b) -> p a b", b=128))) if a == 0 else
                    (lambda e: e.tensor_copy(out=attnT[:, a * 4:(a + 1) * 4, qt * 128:(qt + 1) * 128],
                                             in_=ps[b][:].rearrange("p (a b) -> p a b", b=128)))))(),
                     reads=[psr[b]], writes=[attnT_res])

    plan_barrier(P)
    h2T = es.enter_context(nc.sbuf_tensor("h2T", [128, 8, NQ], BF16))
    h2T_res = Res()
    with ExitStack() as mg:
        wb = mg.enter_context(nc.sbuf_tensor("wb", [128, 8, D], BF16))
        wbr = Res()
        load_w(wb, wbr, w_ba, 0, 384, 0, D)
        for kf in range(3):
            P.dma("pool", (lambda kf=kf: lambda e: e.dma_start(out=wb[:, 3 + kf, :], in_=w_bd[kf * 128:(kf + 1) * 128, :]))(), writes=[wbr])
        for kf in range(2):
            P.dma("pool", (lambda kf=kf: lambda e: e.dma_start(out=wb[:, 6 + kf, :], in_=w_bm[kf * 128:(kf + 1) * 128, :]))(), writes=[wbr])
        wo = mg.enter_context(nc.sbuf_tensor("wo", [128, 8, D], BF16))
        wor = Res()
        load_w(wo, wor, w_out, 0, 1024, 0, D)
        mergedT = mg.enter_context(nc.sbuf_tensor("mergedT", [128, 8, NQ], BF16))
        mres = Res()
        m1 = mg.enter_context(ExitStack())
        sg_r = ring("sg", [128, 3, NQ], BF16, 2, m1)
        t1_r = ring("t1", [128, 512], F32, 2, m1)
        t2_r = ring("t2", [128, 512], F32, 2, m1)
        groups = [(0, 512), (512, 512), (1024, 512), (1536, 512), (2048, 128)]
        cnt_ = 0
        for fc in range(8):
            sg, sgr = sg_r.next()
            for b3 in range(3):
                P.dma("sp" if b3 != 1 else "act", (lambda sg=sg, b3=b3, fc=fc: lambda e: e.dma_start(
                    out=sg[:, b3, :], in_=sig_d[b3 * 8 + fc]))(), writes=[sgr])
            for (g0, gn) in groups:
                bb = (cnt_ % 2) * 3
                cnt_ += 1
                for b3, (k0, nk) in enumerate(((0, 3), (3, 3), (6, 2))):
                    for k in range(nk):
                        P.op("pe", (lambda bb=bb, b3=b3, k0=k0, k=k, nk=nk, fc=fc, g0=g0, gn=gn: lambda e: e.matmul(
                            ps[bb + b3][:, 0:gn], lhsT=wb[:, k0 + k, fc * 128:(fc + 1) * 128], rhs=attnT[:, k0 + k, g0:g0 + gn],
                            start=(k == 0), stop=(k == nk - 1)))(), reads=[wbr, attnT_res], writes=[psr[bb + b3]])
                t1, t1r = t1_r.next()
                t2, t2r = t2_r.next()
                P.op("dve", (lambda t1=t1, bb=bb, sg=sg, g0=g0, gn=gn: lambda e: e.tensor_tensor(
                    out=t1[:, 0:gn], in0=ps[bb][:, 0:gn], in1=sg[:, 0, g0:g0 + gn], op=ALU.mult))(),
                     reads=[psr[bb], sgr], writes=[t1r])
                P.op("dve", (lambda t2=t2, bb=bb, sg=sg, g0=g0, gn=gn: lambda e: e.tensor_tensor(
                    out=t2[:, 0:gn], in0=ps[bb + 1][:, 0:gn], in1=sg[:, 1, g0:g0 + gn], op=ALU.mult))(),
                     reads=[psr[bb + 1], sgr], writes=[t2r])
                P.op("dve", (lambda t1=t1, t2=t2, gn=gn: lambda e: e.tensor_tensor(
                    out=t1[:, 0:gn], in0=t1[:, 0:gn], in1=t2[:, 0:gn], op=ALU.add))(), reads=[t1r, t2r], writes=[t1r])
                P.op("dve", (lambda t2=t2, bb=bb, sg=sg, g0=g0, gn=gn: lambda e: e.tensor_tensor(
                    out=t2[:, 0:gn], in0=ps[bb + 2][:, 0:gn], in1=sg[:, 2, g0:g0 + gn], op=ALU.mult))(),
                     reads=[psr[bb + 2], sgr, t1r], writes=[t2r])
                P.op("dve", (lambda t1=t1, t2=t2, fc=fc, g0=g0, gn=gn: lambda e: e.tensor_tensor(
                    out=mergedT[:, fc, g0:g0 + gn], in0=t1[:, 0:gn], in1=t2[:, 0:gn], op=ALU.add))(),
                     reads=[t1r, t2r], writes=[mres])

        m1.close()
        plan_barrier(P)
        gffn = mg.enter_context(nc.sbuf_tensor("gffn", [128, D], F32))
        P.dma("sp", lambda e: e.dma_start(out=gffn[:], in_=g_ffn), writes=[cres])
        xq_r = ring("xq", [128, D], F32, 3, mg)
        x1_r = ring("x1t", [128, D], F32, 5, mg)
        jk2_r = ring("jk2", [128, D], BF16, 1, mg)
        ss2_r = ring("ss2", [128, 1], F32, 5, mg)
        xn2_r = ring("xn2", [128, D], BF16, 2, mg)
        obk = [0]

        def tileO(qt):
            xq, xqr = xq_r.next()
            P.dma("sp", (lambda xq=xq, qt=qt: lambda e: e.dma_start(out=xq[:], in_=xwin[(16 + qt) * 128:(17 + qt) * 128, :]))(),
                  writes=[xqr])
            x1t, x1r = x1_r.next()
            for half in range(2):
                b = 2 + obk[0] % 6
                obk[0] += 1
                for fc in range(8):
                    P.op("pe", (lambda b=b, fc=fc, qt=qt, half=half: lambda e: e.matmul(
                        ps[b][:], lhsT=mergedT[:, fc, qt * 128:(qt + 1) * 128], rhs=wo[:, fc, half * 512:(half + 1) * 512],
                        start=(fc == 0), stop=(fc == 7)))(), reads=[mres, wor], writes=[psr[b]])
                P.op("dve", (lambda b=b, x1t=x1t, xq=xq, half=half: lambda e: e.tensor_tensor(
                    out=x1t[:, half * 512:(half + 1) * 512], in0=ps[b][:], in1=xq[:, half * 512:(half + 1) * 512], op=ALU.add))(),
                     reads=[psr[b], xqr], writes=[x1r])
            yield
            P.dma("act", (lambda x1t=x1t, qt=qt: lambda e: e.dma_start(out=x1_d[qt * 128:(qt + 1) * 128, :], in_=x1t[:]))(),
                  reads=[x1r])
            jk, jkr = jk2_r.next()
            ss, ssr = ss2_r.next()
            P.op("act", (lambda jk=jk, x1t=x1t, ss=ss: lambda e: e.activation(out=jk[:], in_=x1t[:], func=AF.Square, accum_out=ss[:]))(),
                 reads=[x1r], writes=[jkr, ssr])
            P.op("act", lambda e: e.copy(out=fence[:, 0:1], in_=fence[:, 1:2]), writes=[ssr])
            P.op("act", (lambda ss=ss: lambda e: e.activation(out=ss[:], in_=ss[:], func=AF.Sqrt, scale=1.0 / D, bias=epsc[:, 0:1]))(),
                 reads=[ssr, cres], writes=[ssr])
            P.op("dve", (lambda ss=ss: lambda e: e.reciprocal(out=ss[:], in_=ss[:]))(), reads=[ssr], writes=[ssr])
            yield
            xn, xnr = xn2_r.next()
            P.op("dve", (lambda xn=xn, x1t=x1t, ss=ss: lambda e: e.scalar_tensor_tensor(
                out=xn[:], in0=x1t[:], scalar=ss[:, 0:1], in1=gffn[:], op0=ALU.mult, op1=ALU.mult))(),
                 reads=[x1r, ssr, cres], writes=[xnr])
            for a in range(2):
                b = a
                for k in range(4):
                    fc = a * 4 + k
                    P.op("pe", (lambda xn=xn, fc=fc, b=b, k=k: lambda e: e.matmul(
                        ps[b][:, k * 128:(k + 1) * 128], lhsT=xn[:, fc * 128:(fc + 1) * 128], rhs=ident[:],
                        start=True, stop=True))(), reads=[xnr, cres], writes=[psr[b]])
                if a == 0:
                    P.op("act", (lambda qt=qt, b=b: lambda e: e.copy(
                        out=h2T[:, 0:4, qt * 128:(qt + 1) * 128], in_=ps[b][:].rearrange("p (a b) -> p a b", b=128)))(),
                         reads=[psr[b]], writes=[h2T_res])
                else:
                    P.op("dve", (lambda qt=qt, b=b: lambda e: e.tensor_copy(
                        out=h2T[:, 4:8, qt * 128:(qt + 1) * 128], in_=ps[b][:].rearrange("p (a b) -> p a b", b=128)))(),
                         reads=[psr[b]], writes=[h2T_res])
        run_pipeline((tileO(qt) for qt in range(NQT)), 3)

    plan_barrier(P)
    actT = es.enter_context(nc.sbuf_tensor("actT", [128, 22, TOK], BF16))
    actT_res = Res()
    cws = es.enter_context(nc.sbuf_tensor("cws", [128, 44, 4], F32))
    hval = es.enter_context(nc.sbuf_tensor("hval", [128, 1], F32))
    P.dma("sp", lambda e: e.dma_start(out=cws[:], in_=cw), writes=[cres])
    P.dma("sp", lambda e: e.dma_start(out=hval[:], in_=c_hvalid), writes=[cres])
    with ExitStack() as up:
        wu_r = ring("wu", [128, 8, 256], BF16, 3, up)
        ext_r = [ring("extg", [128, 514], F32, 3, up), ring("extv", [128, 514], F32, 3, up)]
        a_r = [ring("ag", [128, 512], F32, 3, up), ring("av", [128, 512], F32, 3, up)]
        s_r = ring("sil", [128, 512], F32, 2, up)
        wsrc = w_up.rearrange("(kf p) c -> p kf c", p=128)
        bi = [0]
        wus = {}

        def stepU(k, tg):
            if tg == 0:
                wu, wur = wu_r.next()
                wus[k] = (wu, wur)
                for part in range(2):
                    c0 = part * DFF + k * 128
                    P.dma("pool", (lambda wu=wu, part=part, c0=c0: lambda e: e.dma_start(
                        out=wu[:, :, part * 128:(part + 1) * 128], in_=wsrc[:, :, c0:c0 + 128]))(), writes=[wur])
            wu, wur = wus[k]
            exts = []
            for part in range(2):
                ext, extr = ext_r[part].next()
                hc0 = 126 + tg * 512
                b = 2 + bi[0] % 6
                bi[0] += 1
                for kf in range(8):
                    P.op("pe", (lambda b=b, wu=wu, part=part, kf=kf, hc0=hc0: lambda e: e.matmul(
                        ps[b][:, 0:2], lhsT=wu[:, kf, part * 128:(part + 1) * 128], rhs=h2T[:, kf, hc0:hc0 + 2],
                        start=(kf == 0), stop=(kf == 7)))(), reads=[wur, h2T_res], writes=[psr[b]])
                if tg == 0:
                    P.op("act", (lambda b=b, ext=ext: lambda e: e.activation(out=ext[:, 0:2], in_=ps[b][:, 0:2], func=AF.Identity,
                                                                             scale=hval[:, 0:1]))(),
                         reads=[psr[b], cres], writes=[extr])
                else:
                    P.op("act", (lambda b=b, ext=ext: lambda e: e.copy(out=ext[:, 0:2], in_=ps[b][:, 0:2]))(),
                         reads=[psr[b]], writes=[extr])
                b = 2 + bi[0] % 6
                bi[0] += 1
                for kf in range(8):
                    P.op("pe", (lambda b=b, wu=wu, part=part, kf=kf, tg=tg: lambda e: e.matmul(
                        ps[b][:], lhsT=wu[:, kf, part * 128:(part + 1) * 128],
                        rhs=h2T[:, kf, 128 + tg * 512:128 + (tg + 1) * 512], start=(kf == 0), stop=(kf == 7)))(),
                         reads=[wur, h2T_res], writes=[psr[b]])
                P.op("act", (lambda b=b, ext=ext: lambda e: e.copy(out=ext[:, 2:514], in_=ps[b][:]))(),
                     reads=[psr[b]], writes=[extr])
                ch = part * 22 + k
                a_, ar_ = a_r[part].next()
                P.op("act", (lambda b=b, a_=a_, ch=ch: lambda e: e.activation(out=a_[:], in_=ps[b][:], func=AF.Identity,
                                                                             scale=cws[:, ch, 2:3], bias=cws[:, ch, 3:4]))(),
                     reads=[psr[b], cres], writes=[ar_])
                exts.append((ext, extr, a_, ar_))
            yield
            outs = []
            for part in range(2):
                ext, extr, a_, ar_ = exts[part]
                ch = part * 22 + k
                P.op("dve", (lambda a_=a_, ext=ext, ch=ch: lambda e: e.scalar_tensor_tensor(
                    out=a_[:], in0=ext[:, 1:513], scalar=cws[:, ch, 1:2], in1=a_[:], op0=ALU.mult, op1=ALU.add))(),
                     reads=[extr, cres, ar_], writes=[ar_])
                P.op("dve", (lambda a_=a_, ext=ext, ch=ch: lambda e: e.scalar_tensor_tensor(
                    out=a_[:], in0=ext[:, 0:512], scalar=cws[:, ch, 0:1], in1=a_[:], op0=ALU.mult, op1=ALU.add))(),
                     reads=[extr, cres, ar_], writes=[ar_])
                outs.append((a_, ar_))
            yield
            sl, slr = s_r.next()
            (ag, agr), (av, avr) = outs
            P.op("act", (lambda sl=sl, ag=ag: lambda e: e.activation(out=sl[:], in_=ag[:], func=AF.Silu))(), reads=[agr], writes=[slr])
            P.op("dve", (lambda sl=sl, av=av, k=k, tg=tg: lambda e: e.tensor_tensor(
                out=actT[:, k, tg * 512:(tg + 1) * 512], in0=sl[:], in1=av[:], op=ALU.mult))(),
                 reads=[slr, avr], writes=[actT_res])
        run_pipeline((stepU(k, tg) for k in range(22) for tg in range(4)), 3)

    plan_barrier(P)
    with ExitStack() as dn:
        wdA = attnT[:].rearrange("p a b -> p (a b)").rearrange("p (k c) -> p k c", c=D)
        wdB = h2T[:].rearrange("p a b -> p (a b)")[:, 0:5 * D].rearrange("p (k c) -> p k c", c=D)

        def wdk(k):
            return (wdA[:, k, :], attnT_res) if k < 17 else (wdB[:, k - 17, :], h2T_res)
        for k in range(22):
            dst_, res_ = wdk(k)
            P.dma("pool", (lambda dst_=dst_, k=k: lambda e: e.dma_start(out=dst_, in_=w_dn[k * 128:(k + 1) * 128, :]))(),
                  writes=[res_])
        x1b_r = ring("x1b", [128, D], F32, 4, dn)
        yo_r = ring("yo", [128, D], F32, 2, dn)
        dbk = [0]

        def tileD(t):
            x1b, x1br = x1b_r.next()
            P.dma("sp", (lambda x1b=x1b, t=t: lambda e: e.dma_start(out=x1b[:], in_=x1_d[(1 + t) * 128:(2 + t) * 128, :]))(),
                  writes=[x1br])
            yield
            yo, yor = yo_r.next()
            for half in range(2):
                b = dbk[0] % 8
                dbk[0] += 1
                for k in range(22):
                    wk_, wkr_ = wdk(k)
                    P.op("pe", (lambda b=b, k=k, t=t, half=half, wk_=wk_: lambda e: e.matmul(
                        ps[b][:], lhsT=actT[:, k, t * 128:(t + 1) * 128], rhs=wk_[:, half * 512:(half + 1) * 512],
                        start=(k == 0), stop=(k == 21)))(), reads=[actT_res, wkr_], writes=[psr[b]])
                P.op("dve" if half == 0 else "act", (lambda b=b, yo=yo, x1b=x1b, half=half: (
                    (lambda e: e.tensor_tensor(out=yo[:, half * 512:(half + 1) * 512], in0=ps[b][:],
                                               in1=x1b[:, half * 512:(half + 1) * 512], op=ALU.add))))(),
                     reads=[psr[b], x1br], writes=[yor]) if half == 0 else \
                    P.op("dve", (lambda b=b, yo=yo, x1b=x1b, half=half: lambda e: e.tensor_tensor(
                        out=yo[:, half * 512:(half + 1) * 512], in0=ps[b][:], in1=x1b[:, half * 512:(half + 1) * 512], op=ALU.add))(),
                        reads=[psr[b], x1br], writes=[yor])
            P.dma("act", (lambda yo=yo, t=t: lambda e: e.dma_start(out=out_d[t * 128:(t + 1) * 128, :], in_=yo[:]))(), reads=[yor])
        run_pipeline((tileD(t) for t in range(16)), 3)

    SEM = {}
    for e_ in Plan.ENG:
        SEM[e_] = es.enter_context(nc.semaphore(f"s_{e_}"))
    for q in ("sp", "act", "pool"):
        for i in range(NDS):
            SEM[f"d_{q}_{i}"] = es.enter_context(nc.semaphore(f"d_{q}_{i}"))

    def replay(eng, e):
        for waits, fn, dkey in P.prog[eng]:
            for k, v in waits:
                e.wait_ge(SEM[k], v)
            if fn is None:
                continue
            ins = fn(e)
            if dkey is not None:
                ins.then_inc(SEM[dkey], 16)
            else:
                ins.then_inc(SEM[eng], 1)
        if eng in P.dq:
            for i in range(NDS):
                v = P.dq[eng]["val"][i]
                if v > 0:
                    e.wait_ge(SEM[f"d_{eng}_{i}"], v)

    with nc.Block() as block:
        @block.tensor
        def _(e):
            replay("pe", e)

        @block.scalar
        def _(e):
            replay("act", e)

        @block.vector
        def _(e):
            replay("dve", e)

        @block.gpsimd
        def _(e):
            replay("pool", e)

        @block.sync
        def _(e):
            replay("sp", e)
    es.close()
    return nc


def _host_consts(c):
    bf = ml_dtypes.bfloat16
    k = np.arange(128)[:, None]
    q = np.arange(128)[None, :]
    tri = (k <= q).astype(np.float32)
    mdil = np.zeros((128, 17, 128), np.float32)
    for dlt in range(17):
        dist = dlt * 128 + q - k
        m = ((dist >= 0) & (dist <= 128)).astype(np.float32)
        m += ((dist >= 0) & (dist <= 512) & (dist % 4 == 0))
        m += ((dist >= 0) & (dist <= 2048) & (dist % 16 == 0))
        mdil[:, dlt, :] = m
    tmat = (np.arange(S)[None, :] // 256 == np.arange(64)[:, None]).astype(np.float32)
    invf = (10000.0 ** (-np.arange(32, dtype=np.float32) / 32)).astype(np.float32)
    invf2 = np.broadcast_to(np.concatenate([invf, invf])[None], (128, 64))
    offs = np.broadcast_to(np.concatenate([np.zeros(32, np.float32), np.full(32, np.pi / 2, np.float32)])[None], (128, 64))
    gmask = np.zeros((128, NQT, 64), np.float32)
    for qt in range(NQT):
        tok0 = TOK * c - 128 + qt * 128
        own = tok0 // 256 if tok0 >= 0 else -1
        gmask[:, qt, :] = np.where(np.arange(64) < own, 0.0, -1e9)[None]
    kvalid = np.ones((128, NWT), np.float32)
    for wt in range(NWT):
        if TOK * c - 2176 + wt * 128 < 0:
            kvalid[:, wt] = 0.0
    hvalid = np.full((128, 1), 0.0 if c == 0 else 1.0, np.float32)
    return dict(c_ident=np.eye(128, dtype=np.float32).astype(bf), c_tri=tri.astype(bf), c_mdil=mdil.astype(bf),
                c_tmat=tmat.astype(bf), c_invf=np.ascontiguousarray(invf2), c_offs=np.ascontiguousarray(offs),
                c_gmask=gmask, c_kvalid=kvalid, c_hvalid=hvalid)


def kernel(x, mem, positions, mix_norm_g, mem_norm_g, w_in, moba_q_norm_g, moba_k_norm_g,
           dil_q_norm_g, dil_k_norm_g, mem_q_norm_g, mem_k_norm_g, w_mem_kv,
           w_branch_moba, w_branch_dil, w_branch_mem, w_out, ffn_norm_g,
           w_ffn_up, ffn_conv_w, ffn_conv_b, w_ffn_down):
    f = lambda a: np.ascontiguousarray(np.asarray(a, dtype=np.float32))
    x2 = f(x)[0]
    pos = np.ascontiguousarray(np.asarray(positions, dtype=np.int32)[0])
    bc = lambda v, n=128: np.ascontiguousarray(np.broadcast_to(f(v).reshape(1, -1), (n, f(v).size)))
    g6 = np.stack([bc(g[0]) for g in (moba_q_norm_g, moba_k_norm_g, dil_q_norm_g, dil_k_norm_g, mem_q_norm_g, mem_k_norm_g)], axis=1)
    cwt = np.concatenate([f(ffn_conv_w)[0], f(ffn_conv_b)[0][None]], axis=0)
    cwt = np.ascontiguousarray(cwt.reshape(4, 44, 128).transpose(2, 1, 0))
    shared = dict(xall=x2, pos_allT=np.ascontiguousarray(pos.reshape(S // 128, 128).T), mem=f(mem)[0], w_in=f(w_in)[0], w_mkv=f(w_mem_kv)[0],
                  w_ba=f(w_branch_moba)[0], w_bd=f(w_branch_dil)[0], w_bm=f(w_branch_mem)[0], w_out=f(w_out)[0],
                  w_up=f(w_ffn_up)[0], w_dn=f(w_ffn_down)[0], cw=cwt, g_mix=bc(mix_norm_g[0]), g_mem=bc(mem_norm_g[0]),
                  g_ffn=bc(ffn_norm_g[0]), g6=np.ascontiguousarray(g6))
    in_maps = []
    for c in range(NCORE):
        lo = TOK * c - 2176
        xw = np.zeros((NW, D), np.float32)
        pw = np.zeros((NW, 1), np.int32)
        a = max(lo, 0)
        xw[a - lo:] = x2[a:lo + NW]
        pw[a - lo:, 0] = pos[a:lo + NW]
        m = dict(shared)
        m.update(xwin=xw, pos_winT=np.ascontiguousarray(pw.reshape(NWT, 128).T))
        m.update(_host_consts(c))
        in_maps.append(m)
    nc = build_nc()
    res = run_bass_kernel_spmd(nc, in_maps, core_ids=list(range(NCORE)))
    out = np.concatenate([np.asarray(r["out"], dtype=np.float32) for r in res.results], axis=0)
    return out.reshape(1, S, D)
```
